# Optimizing a Trainium2 kernel written in Bass

```python
import math
import jax, jax.numpy as jnp
from jax import lax
import numpy as np

D_MODEL = 1024
BATCH = 1
SEQ = 16384
DEPTH = 2

PLE_DIM = 256
N_MIXERS = 4
GROUP_W = 256
MIX_W = N_MIXERS * GROUP_W
EPS = 1e-6
CONV_K = 4
RET_HEADS = 4
RET_HD = GROUP_W // RET_HEADS
RET_CHUNK = 128
ROPE_BASE = 10000.0
LRU_BLOCKS = 4
LRU_BW = GROUP_W // LRU_BLOCKS
LRU_C = 8.0
S5_GW = 16
S5_GROUPS = GROUP_W // S5_GW
S5_STATE = 64
GDN_HEADS = 4
GDN_HD = GROUP_W // GDN_HEADS
GDN_CHUNK = 64
PEER_HEADS = 8
PEER_NKEYS = 128
PEER_N = PEER_NKEYS * PEER_NKEYS
PEER_DKEY = 128
PEER_HALF = PEER_DKEY // 2
PEER_TOPK = 16
PEER_BLOCK = 128

IN_SIZES = (GROUP_W, GROUP_W, GROUP_W, GROUP_W,
            GROUP_W, GROUP_W,
            GROUP_W,
            3 * GROUP_W, GDN_HEADS, GDN_HEADS, GROUP_W)
IN_W = sum(IN_SIZES)
IN_SPLITS = [int(c) for c in np.cumsum(IN_SIZES)[:-1]]

kernel_name = 'hybrid_retention_rglru_s5_gdn_peer'


def rmsnorm(x, g):
    xf = x.astype(jnp.float32)
    y = xf * lax.rsqrt(jnp.mean(xf * xf, axis=-1, keepdims=True) + EPS)
    return (y * g.astype(jnp.float32)).astype(x.dtype)


def causal_dwconv(x, w):
    s = x.shape[1]
    xp = jnp.pad(x, ((0, 0), (CONV_K - 1, 0), (0, 0)))
    return sum(w[j] * xp[:, j:j + s] for j in range(CONV_K))


def rotary(t, positions):
    half = t.shape[-1] // 2
    inv_freq = ROPE_BASE ** (-jnp.arange(half, dtype=jnp.float32) / half)
    ang = positions.astype(jnp.float32)[..., None] * inv_freq
    cos = jnp.cos(ang)[:, :, None, :]
    sin = jnp.sin(ang)[:, :, None, :]
    t1, t2 = t[..., :half], t[..., half:]
    return jnp.concatenate([t1 * cos - t2 * sin, t2 * cos + t1 * sin], axis=-1)


def linear_scan(a, b):
    def combine(e1, e2):
        return e1[0] * e2[0], e2[0] * e1[1] + e2[1]
    return lax.associative_scan(combine, (a, b), axis=1)[1]


def retention(q, k, v, g, gn_w, positions):
    bsz, s, _ = q.shape
    n = s // RET_CHUNK
    shp = (bsz, s, RET_HEADS, RET_HD)
    q = rotary(q.astype(jnp.float32).reshape(shp), positions) * RET_HD ** -0.5
    k = rotary(k.astype(jnp.float32).reshape(shp), positions)
    v = v.astype(jnp.float32).reshape(shp)
    to_chunks = lambda t: t.reshape(bsz, n, RET_CHUNK, RET_HEADS, RET_HD).transpose(1, 0, 3, 2, 4)
    log_gamma = jnp.log(1.0 - 2.0 ** (-5.0 - jnp.arange(RET_HEADS, dtype=jnp.float32)))
    idx = jnp.arange(RET_CHUNK, dtype=jnp.float32)
    diff = idx[:, None] - idx[None, :]
    causal = diff >= 0
    decay_mask = jnp.where(causal, jnp.exp(log_gamma[:, None, None] * jnp.where(causal, diff, 0.0)), 0.0)
    q_decay = jnp.exp(log_gamma[:, None] * (idx + 1.0))[None, :, :, None]
    k_decay = jnp.exp(log_gamma[:, None] * (RET_CHUNK - 1.0 - idx))[None, :, :, None]
    chunk_decay = jnp.exp(log_gamma * RET_CHUNK)[None, :, None, None]

    def step(state, inp):
        qc, kc, vc = inp
        scores = jnp.einsum('bhid,bhjd->bhij', qc, kc) * decay_mask
        out = (jnp.einsum('bhij,bhjd->bhid', scores, vc)
               + jnp.einsum('bhid,bhde->bhie', qc, state) * q_decay)
        state = state * chunk_decay + jnp.einsum('bhjd,bhje->bhde', kc * k_decay, vc)
        return state, out

    state0 = jnp.zeros((bsz, RET_HEADS, RET_HD, RET_HD), jnp.float32)
    _, o = lax.scan(step, state0, (to_chunks(q), to_chunks(k), to_chunks(v)))
    o = o.transpose(1, 0, 3, 2, 4).reshape(shp)
    mu = jnp.mean(o, axis=-1, keepdims=True)
    var = jnp.mean(jnp.square(o - mu), axis=-1, keepdims=True)
    o = ((o - mu) * lax.rsqrt(var + EPS)).reshape(bsz, s, GROUP_W) * gn_w.astype(jnp.float32)
    return (jax.nn.silu(g.astype(jnp.float32)) * o).astype(g.dtype)


def rglru(gate_in, x_in, conv_w, conv_b, w_a, b_a, w_x, b_x, lam):
    bsz, s, _ = x_in.shape
    f32 = jnp.float32
    xb = (causal_dwconv(x_in, conv_w) + conv_b).astype(f32)
    xblk = xb.reshape(bsz, s, LRU_BLOCKS, LRU_BW)
    r = jax.nn.sigmoid(jnp.einsum('bsnc,ncd->bsnd', xblk, w_a.astype(f32)).reshape(bsz, s, GROUP_W) + b_a.astype(f32))
    i = jax.nn.sigmoid(jnp.einsum('bsnc,ncd->bsnd', xblk, w_x.astype(f32)).reshape(bsz, s, GROUP_W) + b_x.astype(f32))
    log_a = -LRU_C * r * jax.nn.softplus(-lam.astype(f32))
    a = jnp.exp(log_a)
    b = jnp.sqrt(-jnp.expm1(2.0 * log_a)) * (i * xb)
    h = linear_scan(a, b)
    return (jax.nn.gelu(gate_in.astype(f32)) * h).astype(x_in.dtype)


def s5(u, a_re, a_im, b_re, b_im, c_re, c_im, d, log_dt, glu_w, glu_b):
    bsz, s, _ = u.shape
    f32 = jnp.float32
    uf = u.astype(f32)
    a_re, a_im = a_re.astype(f32), a_im.astype(f32)
    dt = jnp.exp(log_dt.astype(f32))[:, None]
    mag = jnp.exp(a_re * dt)
    ang = a_im * dt
    ab_re, ab_im = mag * jnp.cos(ang), mag * jnp.sin(ang)
    den = a_re * a_re + a_im * a_im
    p_re, p_im = ab_re - 1.0, ab_im
    f_re = (p_re * a_re + p_im * a_im) / den
    f_im = (p_im * a_re - p_re * a_im) / den
    b_re, b_im = b_re.astype(f32), b_im.astype(f32)
    bb_re = f_re[..., None] * b_re - f_im[..., None] * b_im
    bb_im = f_re[..., None] * b_im + f_im[..., None] * b_re
    ug = uf.reshape(bsz, s, S5_GROUPS, S5_GW)
    bu_re = jnp.einsum('bsgc,gnc->bsgn', ug, bb_re)
    bu_im = jnp.einsum('bsgc,gnc->bsgn', ug, bb_im)
    at_re = jnp.broadcast_to(ab_re, bu_re.shape)
    at_im = jnp.broadcast_to(ab_im, bu_im.shape)

    def combine(e1, e2):
        ar1, ai1, br1, bi1 = e1
        ar2, ai2, br2, bi2 = e2
        return (ar2 * ar1 - ai2 * ai1, ar2 * ai1 + ai2 * ar1,
                ar2 * br1 - ai2 * bi1 + br2, ar2 * bi1 + ai2 * br1 + bi2)

    _, _, x_re, x_im = lax.associative_scan(combine, (at_re, at_im, bu_re, bu_im), axis=1)
    y = (jnp.einsum('bsgn,gcn->bsgc', x_re, c_re.astype(f32))
         - jnp.einsum('bsgn,gcn->bsgc', x_im, c_im.astype(f32)))
    y = y.reshape(bsz, s, GROUP_W) + d.astype(f32) * uf
    zz = jax.nn.gelu(y) @ glu_w.astype(f32) + glu_b.astype(f32)
    out = zz[..., :GROUP_W] * jax.nn.sigmoid(zz[..., GROUP_W:])
    return out.astype(u.dtype)


def gated_deltanet(qkv, b_logit, a_logit, g, conv_w, a_log, dt_bias, norm_w):
    bsz, s, _ = qkv.shape
    n = s // GDN_CHUNK
    f32 = jnp.float32
    qkv = jax.nn.silu(causal_dwconv(qkv, conv_w).astype(f32)).reshape(bsz, s, 3, GDN_HEADS, GDN_HD)
    l2 = lambda t: t * lax.rsqrt(jnp.sum(t * t, axis=-1, keepdims=True) + EPS)
    q = l2(qkv[:, :, 0]) * GDN_HD ** -0.5
    k = l2(qkv[:, :, 1])
    v = qkv[:, :, 2]
    beta = jax.nn.sigmoid(b_logit.astype(f32))
    log_alpha = -jnp.exp(a_log.astype(f32)) * jax.nn.softplus(a_logit.astype(f32) + dt_bias.astype(f32))
    chunk = lambda t: jnp.moveaxis(t.reshape(bsz, n, GDN_CHUNK, GDN_HEADS, *t.shape[3:]), 3, 1)
    qc, kc, vc = chunk(q), chunk(k), chunk(v)
    bc = chunk(beta)
    gcum = jnp.cumsum(chunk(log_alpha), axis=-1)
    idx = jnp.arange(GDN_CHUNK)
    incl = idx[:, None] >= idx[None, :]
    strict = idx[:, None] > idx[None, :]
    diff = gcum[..., :, None] - gcum[..., None, :]
    lmask = jnp.where(incl, jnp.exp(jnp.where(incl, diff, 0.0)), 0.0)
    kb = kc * bc[..., None]
    lower = jnp.where(strict, jnp.einsum('bhnid,bhnjd->bhnij', kb, kc) * lmask, 0.0)
    eye = jnp.eye(GDN_CHUNK, dtype=f32)
    rhs = jnp.concatenate([vc * bc[..., None], kb * jnp.exp(gcum)[..., None]], axis=-1)
    sol = lax.linalg.triangular_solve(eye + lower, rhs, left_side=True, lower=True)
    u_c, w_c = sol[..., :GDN_HD], sol[..., GDN_HD:]
    attn = jnp.where(incl, jnp.einsum('bhnid,bhnjd->bhnij', qc, kc) * lmask, 0.0)
    q_dec = qc * jnp.exp(gcum)[..., None]
    k_dec = kc * jnp.exp(gcum[..., -1:] - gcum)[..., None]
    c_dec = jnp.exp(gcum[..., -1])[..., None, None]

    def step(state, inp):
        a_i, u_i, w_i, qd_i, kd_i, cd_i = inp
        v_new = u_i - jnp.einsum('bhcd,bhde->bhce', w_i, state)
        out = jnp.einsum('bhcd,bhde->bhce', qd_i, state) + jnp.einsum('bhij,bhje->bhie', a_i, v_new)
        state = state * cd_i + jnp.einsum('bhcd,bhce->bhde', kd_i, v_new)
        return state, out

    xs = tuple(jnp.moveaxis(t, 2, 0) for t in (attn, u_c, w_c, q_dec, k_dec, c_dec))
    state0 = jnp.zeros((bsz, GDN_HEADS, GDN_HD, GDN_HD), f32)
    _, o = lax.scan(step, state0, xs)
    o = o.transpose(1, 0, 3, 2, 4).reshape(bsz, s, GDN_HEADS, GDN_HD)
    o = rmsnorm(o, norm_w)
    gate = jax.nn.silu(g.astype(f32)).reshape(bsz, s, GDN_HEADS, GDN_HD)
    return (o * gate).reshape(bsz, s, GROUP_W).astype(g.dtype)


def peer(z, w_q, sub_keys, u_tab, v_tab):
    bsz, s, d = z.shape
    q = (z @ w_q).reshape(bsz, s, PEER_HEADS, 2, PEER_HALF)
    sc = jnp.einsum('bshpd,hpkd->bshpk', q, sub_keys).astype(jnp.float32)
    s1, i1 = lax.top_k(sc[..., 0, :], PEER_TOPK)
    s2, i2 = lax.top_k(sc[..., 1, :], PEER_TOPK)
    n_cand = PEER_TOPK * PEER_TOPK
    cand_s = (s1[..., :, None] + s2[..., None, :]).reshape(bsz, s, PEER_HEADS, n_cand)
    cand_id = (i1[..., :, None] * PEER_NKEYS + i2[..., None, :]).reshape(bsz, s, PEER_HEADS, n_cand)
    top_s, top_pos = lax.top_k(cand_s, PEER_TOPK)
    expert_id = jnp.take_along_axis(cand_id, top_pos, axis=-1)
    gates = jax.nn.softmax(top_s, axis=-1).astype(z.dtype)
    n_sel = PEER_HEADS * PEER_TOPK
    nblk = bsz * s // PEER_BLOCK
    zb = z.reshape(nblk, PEER_BLOCK, d)
    eb = expert_id.reshape(nblk, PEER_BLOCK, n_sel)
    gb = gates.reshape(nblk, PEER_BLOCK, n_sel)

    def block(args):
        zt, et, gt = args
        act = jax.nn.gelu(jnp.einsum('tkd,td->tk', u_tab[et], zt))
        return jnp.einsum('tk,tkd->td', gt * act, v_tab[et])

    return lax.map(block, (zb, eb, gb)).reshape(bsz, s, d)


def setup_inputs(seed: int = 0) -> dict:
    key = jax.random.key(seed)
    ks = iter(jax.random.split(key, 48))
    f32 = jnp.float32
    L = DEPTH

    def nrm(shape, scale):
        return jax.random.normal(next(ks), shape, f32) * scale

    def gain(shape):
        return 1.0 + nrm(shape, 0.02)

    def unif(shape, lo, hi):
        return jax.random.uniform(next(ks), shape, f32, lo, hi)

    lru_a = unif((L, GROUP_W), 0.9, 0.999) ** (1.0 / LRU_C)
    gdn_dt = jnp.exp(unif((L, GDN_HEADS), math.log(1e-3), math.log(1e-1)))
    return {
        'x': nrm((BATCH, SEQ, D_MODEL), 1.0),
        'p': nrm((DEPTH, BATCH, SEQ, PLE_DIM), 1.0),
        'positions': jnp.broadcast_to(jnp.arange(SEQ, dtype=jnp.int32)[None, :], (BATCH, SEQ)),
        'mix_norm': gain((L, D_MODEL)),
        'w_in': nrm((L, D_MODEL, IN_W), D_MODEL ** -0.5),
        'ret_gn': gain((L, GROUP_W)),
        'lru_conv_w': nrm((L, CONV_K, GROUP_W), CONV_K ** -0.5),
        'lru_conv_b': nrm((L, GROUP_W), 0.01),
        'lru_w_a': nrm((L, LRU_BLOCKS, LRU_BW, LRU_BW), LRU_BW ** -0.5),
        'lru_b_a': nrm((L, GROUP_W), 0.01),
        'lru_w_x': nrm((L, LRU_BLOCKS, LRU_BW, LRU_BW), LRU_BW ** -0.5),
        'lru_b_x': nrm((L, GROUP_W), 0.01),
        'lru_lambda': jnp.log(lru_a) - jnp.log1p(-lru_a),
        's5_a_re': -0.5 + nrm((L, S5_GROUPS, S5_STATE), 0.01),
        's5_a_im': jnp.pi * jnp.arange(S5_STATE, dtype=f32)[None, None, :] + nrm((L, S5_GROUPS, S5_STATE), 0.01),
        's5_b_re': nrm((L, S5_GROUPS, S5_STATE, S5_GW), (2.0 * S5_GW) ** -0.5),
        's5_b_im': nrm((L, S5_GROUPS, S5_STATE, S5_GW), (2.0 * S5_GW) ** -0.5),
        's5_c_re': nrm((L, S5_GROUPS, S5_GW, S5_STATE), S5_STATE ** -0.5),
        's5_c_im': nrm((L, S5_GROUPS, S5_GW, S5_STATE), S5_STATE ** -0.5),
        's5_d': nrm((L, GROUP_W), 1.0),
        's5_log_dt': unif((L, S5_GROUPS), math.log(1e-3), math.log(1e-1)),
        's5_glu_w': nrm((L, GROUP_W, 2 * GROUP_W), GROUP_W ** -0.5),
        's5_glu_b': nrm((L, 2 * GROUP_W), 0.01),
        'gdn_conv_w': nrm((L, CONV_K, 3 * GROUP_W), CONV_K ** -0.5),
        'gdn_a_log': jnp.log(unif((L, GDN_HEADS), 1.0, 16.0)),
        'gdn_dt_bias': gdn_dt + jnp.log(-jnp.expm1(-gdn_dt)),
        'gdn_norm': gain((L, GDN_HD)),
        'branch_norm': gain((L, N_MIXERS, GROUP_W)),
        'w_out': nrm((L, MIX_W, D_MODEL), MIX_W ** -0.5),
        'ffn_norm': gain((L, D_MODEL)),
        'peer_wq': nrm((L, D_MODEL, PEER_HEADS * PEER_DKEY), D_MODEL ** -0.5),
        'peer_subkeys': nrm((L, PEER_HEADS, 2, PEER_NKEYS, PEER_HALF), PEER_HALF ** -0.5),
        'peer_u': nrm((L, PEER_N, D_MODEL), D_MODEL ** -0.5),
        'peer_v': nrm((L, PEER_N, D_MODEL), (PEER_HEADS * PEER_TOPK) ** -0.5),
        'ple_norm': gain((L, D_MODEL)),
        'ple_wg': nrm((L, D_MODEL, D_MODEL), D_MODEL ** -0.5),
        'ple_wp': nrm((L, PLE_DIM, D_MODEL), PLE_DIM ** -0.5),
        'final_norm': gain((D_MODEL,)),
    }


def reference(x, p, positions, mix_norm, w_in, ret_gn, lru_conv_w, lru_conv_b, lru_w_a, lru_b_a,
              lru_w_x, lru_b_x, lru_lambda, s5_a_re, s5_a_im, s5_b_re, s5_b_im, s5_c_re, s5_c_im,
              s5_d, s5_log_dt, s5_glu_w, s5_glu_b, gdn_conv_w, gdn_a_log, gdn_dt_bias, gdn_norm,
              branch_norm, w_out, ffn_norm, peer_wq, peer_subkeys, peer_u, peer_v, ple_norm,
              ple_wg, ple_wp, final_norm):
    h = x
    for l in range(DEPTH):
        z = rmsnorm(h, mix_norm[l])
        (rq, rk, rv, rg, lg, lx, su, dqkv, db, da, dg) = jnp.split(z @ w_in[l], IN_SPLITS, axis=-1)
        y_ret = retention(rq, rk, rv, rg, ret_gn[l], positions)
        y_lru = rglru(lg, lx, lru_conv_w[l], lru_conv_b[l], lru_w_a[l], lru_b_a[l],
                      lru_w_x[l], lru_b_x[l], lru_lambda[l])
        y_s5 = s5(su, s5_a_re[l], s5_a_im[l], s5_b_re[l], s5_b_im[l], s5_c_re[l], s5_c_im[l],
                  s5_d[l], s5_log_dt[l], s5_glu_w[l], s5_glu_b[l])
        y_gdn = gated_deltanet(dqkv, db, da, dg, gdn_conv_w[l], gdn_a_log[l], gdn_dt_bias[l], gdn_norm[l])
        groups = (y_ret, y_lru, y_s5, y_gdn)
        mixed = jnp.concatenate([rmsnorm(y, branch_norm[l, j]) for j, y in enumerate(groups)], axis=-1)
        h = h + mixed @ w_out[l]
        h = h + peer(rmsnorm(h, ffn_norm[l]), peer_wq[l], peer_subkeys[l], peer_u[l], peer_v[l])
        gate = jax.nn.sigmoid(rmsnorm(h, ple_norm[l]) @ ple_wg[l])
        h = h + (p[l] @ ple_wp[l]) * gate
    return rmsnorm(h, final_norm)
```

```python
import bisect
import contextlib
import math
import numpy as np
import concourse.bass as bass
import concourse.mybir as mybir
from concourse.bass_utils import run_bass_kernel_spmd

F32 = mybir.dt.float32
I32 = mybir.dt.int32
U32 = mybir.dt.uint32
ALU = mybir.AluOpType
AF = mybir.ActivationFunctionType
AX = mybir.AxisListType

ENGS = ['pe', 'act', 'dve', 'pool', 'sp']
NDMA = 6
INF = 1 << 60

NCORES = 8
DBG_STOP = 0
SEQ = 16384
T = SEQ // NCORES
NT = T // 128
NB = T // 512
HALO = 3
TH = T + HALO
D = 1024
EPS = 1e-6
MAGIC = 12582912.0
TWO_PI = 2.0 * math.pi
CW1 = 6.28125
PI_SAFE = 3.1415925
CW2 = float(np.float32(TWO_PI - 6.28125))


class Buf:
    def __init__(self, ap, space, lo, hi):
        self.ap = ap
        self.space = space
        self.lo = lo
        self.hi = hi

    def sub(self, a, b):
        return Buf(self.ap[:, a:b], self.space, self.lo + a, self.lo + b)

    def v(self, ap):
        return Buf(ap, self.space, self.lo, self.hi)

    def r3(self, a):
        return Buf(self.ap.rearrange("p (a b) -> p a b", a=a), self.space, self.lo, self.hi)


class _Space:
    def __init__(self):
        self.b = [-INF, INF]
        self.st = [[None, []]]

    def _split(self, x):
        i = bisect.bisect_right(self.b, x) - 1
        if self.b[i] == x:
            return
        self.b.insert(i + 1, x)
        self.st.insert(i + 1, [self.st[i][0], list(self.st[i][1])])

    def rng(self, lo, hi):
        self._split(lo)
        self._split(hi)
        i = bisect.bisect_left(self.b, lo)
        j = bisect.bisect_left(self.b, hi)
        return self.st[i:j]


class Prog:
    def __init__(self, nc):
        self.nc = nc
        self.ops = {e: [] for e in ENGS}
        self.cnt = {e: 0 for e in ENGS}
        self.seen = {e: {} for e in ENGS}
        self.spaces = {}
        self.dma_cnt = {}
        self.dma_rr = {e: 0 for e in ENGS}
        self.sems = {}
        self.nops = 0

    def _sp(self, name):
        s = self.spaces.get(name)
        if s is None:
            s = self.spaces[name] = _Space()
        return s

    def op(self, eng, fn, reads=(), writes=(), dma=False, cc=False):
        deps = set()
        rsts = []
        wsts = []

        def _bk(b):
            if b.space == 'psum':
                return Buf(None, 'psum', (b.lo // 512) * 512, ((b.hi + 511) // 512) * 512)
            return b
        reads = [_bk(b) for b in reads]
        writes = [_bk(b) for b in writes]
        for b in writes:
            for st in self._sp(b.space).rng(b.lo, b.hi):
                wsts.append(st)
                if st[0] is not None:
                    deps.add(st[0])
                for t in st[1]:
                    deps.add(t)
        for b in reads:
            for st in self._sp(b.space).rng(b.lo, b.hi):
                rsts.append(st)
                if st[0] is not None:
                    deps.add(st[0])
                if b.space == 'psum':
                    for t in st[1]:
                        if t[0][0] != eng:
                            deps.add(t)
        if cc:
            key = (eng, 'cc')
            c = self.dma_cnt.get(key, 0)
            if c > 0:
                deps.add((key, c))
            self.dma_cnt[key] = c + 1
            tok = (key, c + 1)
            inc = (key, 1)
        elif dma:
            slot = self.dma_rr[eng] % NDMA
            self.dma_rr[eng] += 1
            key = (eng, 'd', slot)
            c = self.dma_cnt.get(key, 0)
            if c > 0:
                deps.add((key, 16 * c))
            self.dma_cnt[key] = c + 1
            tok = (key, 16 * (c + 1))
            inc = (key, 16)
        else:
            self.cnt[eng] += 1
            key = (eng, 'c')
            tok = (key, self.cnt[eng])
            inc = (key, 1)
        best = {}
        for (k, v) in deps:
            if eng == 'pe' and k == ('pe', 'c'):
                continue
            if self.seen[eng].get(k, 0) >= v:
                continue
            if best.get(k, 0) < v:
                best[k] = v
        waits = []
        for k, v in best.items():
            self.seen[eng][k] = v
            waits.append((k, v))
        self.ops[eng].append((fn, waits, inc))
        for st in wsts:
            st[0] = tok
            st[1] = []
        wset = set(id(s) for s in wsts)
        for st in rsts:
            if id(st) not in wset:
                st[1].append(tok)
        self.nops += 1
        return tok

    def final_wait_all(self, eng='sp'):
        waits = []
        for e in ENGS:
            if self.cnt[e] > 0:
                waits.append(((e, 'c'), self.cnt[e]))
        for key, c in self.dma_cnt.items():
            waits.append((key, c if key[1] == 'cc' else 16 * c))
        self.ops[eng].append((None, waits, None))

    def emit(self):
        nc = self.nc
        keys = set()
        for e in ENGS:
            for (fn, waits, inc) in self.ops[e]:
                for k, _ in waits:
                    keys.add(k)
                if inc is not None:
                    keys.add(inc[0])
        with contextlib.ExitStack() as st:
            for k in sorted(keys, key=str):
                nm = "s_" + "_".join(str(x) for x in k)
                self.sems[k] = st.enter_context(nc.semaphore(nm))
            block = st.enter_context(nc.Block())

            def run(e):
                def body(engobj):
                    for (fn, waits, inc) in self.ops[e]:
                        for k, v in waits:
                            engobj.wait_ge(self.sems[k], v)
                        if fn is None:
                            continue
                        ins = fn(engobj)
                        ins.then_inc(self.sems[inc[0]], inc[1])
                return body

            block.tensor(run('pe'))
            block.scalar(run('act'))
            block.vector(run('dve'))
            block.gpsimd(run('pool'))
            block.sync(run('sp'))


class Arena:
    def __init__(self, t, name, ncols):
        self.t = t
        self.n = ncols
        self.off = 0
        self.name = name

    def alloc(self, ncols):
        assert self.off + ncols <= self.n, f"arena {self.name} overflow {self.off}+{ncols}>{self.n}"
        b = Buf(self.t[:, self.off:self.off + ncols], self.name, self.off, self.off + ncols)
        self.off += ncols
        return b

    def mark(self):
        return self.off

    def release(self, m):
        self.off = m


def _const_layout():
    off = {}
    cur = [0]

    def add(name, n):
        off[name] = (cur[0], cur[0] + n)
        cur[0] += n

    add('ident', 128); add('ones', 128); add('blk64', 128)
    add('invf', 1); add('sgn', 1)
    add('maskT', 512); add('kdec', 256); add('qdec', 512); add('cd', 128); add('cdT', 128)
    add('tri', 128); add('sel_last', 128); add('maskL', 128); add('maskI', 128)
    add('sel2_0', 128); add('sel2_1', 128)
    add('tau', 516)
    add('pinit', 64)
    return off, cur[0]


CO, NCONST = _const_layout()


def make_consts():
    c = np.zeros((128, NCONST), np.float64)

    def put(name, arr):
        a, b = CO[name]
        c[:, a:b] = arr

    p = np.arange(128)
    put('ident', np.eye(128))
    put('ones', np.ones((128, 128)))
    put('blk64', (p[:, None] // 64 == p[None, :] // 64).astype(np.float64))
    half = 32
    invf = (np.float32(10000.0) ** (-np.arange(half, dtype=np.float32) / np.float32(half))).astype(np.float32)
    put('invf', invf[p % 32].astype(np.float64)[:, None])
    put('sgn', np.where((p % 64) < 32, -1.0, 1.0)[:, None])
    gam = 1.0 - 2.0 ** (-5.0 - np.arange(4))
    lg = np.log(gam)
    i = np.arange(128)
    mt = np.zeros((128, 4, 128))
    for h in range(4):
        dif = i[None, :] - i[:, None]
        mt[:, h, :] = np.where(dif >= 0, np.exp(lg[h] * np.maximum(dif, 0)), 0.0)
    put('maskT', mt.reshape(128, 512))
    kd = np.zeros((128, 4, 64))
    for h in range(4):
        kd[:, h, :] = np.exp(lg[h] * (127.0 - i))[:, None]
    put('kdec', kd.reshape(128, 256))
    qd = np.zeros((128, 4, 128))
    cd = np.zeros((128, 2, 64))
    cdT = np.zeros((128, 2, 64))
    for hp in range(2):
        for par in range(2):
            h = 2 * hp + par
            qd[par * 64:(par + 1) * 64, h, :] = np.exp(lg[h] * (i + 1.0))[None, :]
            cd[par * 64:(par + 1) * 64, hp, :] = np.exp(lg[h] * 128.0)
            cdT[par * 64:(par + 1) * 64, hp, :] = np.exp(lg[h] * float(T))
    put('qdec', qd.reshape(128, 512))
    put('cd', cd.reshape(128, 128))
    put('cdT', cdT.reshape(128, 128))
    same = (p[:, None] // 64 == p[None, :] // 64)
    put('tri', (same & (p[:, None] <= p[None, :])).astype(np.float64))
    put('sel_last', (p[:, None] == (p[None, :] // 64) * 64 + 63).astype(np.float64))
    put('maskL', (same & (p[:, None] > p[None, :])).astype(np.float64))
    put('maskI', (same & (p[None, :] >= p[:, None])).astype(np.float64))
    put('sel2_0', np.repeat((p == 63).astype(np.float64)[:, None], 128, 1))
    put('sel2_1', np.repeat((p == 127).astype(np.float64)[:, None], 128, 1))
    put('tau', np.repeat(np.arange(516, dtype=np.float64)[None, :], 128, 0))
    pin = np.zeros((128, 64))
    pin[p, p % 64] = 1.0
    put('pinit', pin)
    return c.astype(np.float32)


def _pv_layout():
    off = {}
    cur = [0]

    def add(name, n):
        off[name] = (cur[0], cur[0] + n)
        cur[0] += n

    for nm in ('mix_norm', 'ffn_norm', 'ple_norm', 'final_norm'):
        add(nm, 8)
    add('lru_cw', 8); add('lru_cb', 2); add('lru_ba', 2); add('lru_bx', 2); add('lru_lam', 2); add('bn1', 2)
    add('s5_are', 8); add('s5_aim', 8); add('s5_ldt', 8); add('s5_d', 2); add('s5_glub', 4); add('bn2', 2)
    add('gdn_cw', 24)
    return off, cur[0]


PV, NPV = _pv_layout()


def _pr_layout():
    off = {}
    cur = [0]

    def add(name, n):
        off[name] = (cur[0], cur[0] + n)
        cur[0] += n

    add('ret_gn', 256); add('bn0', 256); add('gdn_norm', 64); add('bn3', 256); add('a_log', 4); add('dt_bias', 4)
    return off, cur[0]


PR, NPR = _pr_layout()


def colmajor(v, n):
    return np.ascontiguousarray(np.asarray(v, np.float32).reshape(n, 128).T)


def make_layer_params(inp, l):
    g = lambda k: np.asarray(inp[k][l], np.float32)
    pv = np.zeros((128, NPV), np.float32)

    def put(name, arr):
        a, b = PV[name]
        pv[:, a:b] = arr

    put('mix_norm', colmajor(g('mix_norm'), 8))
    put('ffn_norm', colmajor(g('ffn_norm'), 8))
    put('ple_norm', colmajor(g('ple_norm'), 8))
    put('final_norm', colmajor(np.asarray(inp['final_norm'], np.float32), 8))
    cw = g('lru_conv_w')
    put('lru_cw', np.stack([colmajor(cw[j], 2) for j in range(4)], axis=2).reshape(128, 8))
    put('lru_cb', colmajor(g('lru_conv_b'), 2))
    put('lru_ba', colmajor(g('lru_b_a'), 2))
    put('lru_bx', colmajor(g('lru_b_x'), 2))
    put('lru_lam', colmajor(g('lru_lambda'), 2))
    bn = g('branch_norm')
    put('bn1', colmajor(bn[1], 2))
    put('bn2', colmajor(bn[2], 2))
    put('s5_are', colmajor(g('s5_a_re').reshape(-1), 8))
    put('s5_aim', colmajor(g('s5_a_im').reshape(-1), 8))
    put('s5_ldt', colmajor(np.repeat(g('s5_log_dt'), 64), 8))
    put('s5_d', colmajor(g('s5_d'), 2))
    put('s5_glub', colmajor(g('s5_glu_b'), 4))
    gcw = g('gdn_conv_w')
    put('gdn_cw', np.stack([colmajor(gcw[j], 6) for j in range(4)], axis=2).reshape(128, 24))
    pr = np.zeros((128, NPR), np.float32)

    def putr(name, row):
        a, b = PR[name]
        pr[:, a:b] = np.asarray(row, np.float32)[None, :]

    putr('ret_gn', g('ret_gn')); putr('bn0', bn[0]); putr('gdn_norm', g('gdn_norm')); putr('bn3', bn[3])
    putr('a_log', g('gdn_a_log')); putr('dt_bias', g('gdn_dt_bias'))

    w_in = g('w_in')
    sp = np.cumsum([0, 256, 256, 256, 256, 256, 256, 256, 768, 4, 4, 256])
    rq, rk, rv, rg, lg_, lx, su, dqkv, db, da, dg = [w_in[:, sp[i]:sp[i + 1]] for i in range(11)]

    def swap(w):
        w4 = w.reshape(1024, 4, 2, 32)
        return w4[:, :, ::-1, :].reshape(1024, 256)

    w_fm = np.concatenate([rq, swap(rq), rk, swap(rk), lg_, lx, su, dqkv], axis=1)
    w_tm = np.concatenate([rv, rg, dg, db, da], axis=1)
    assert w_fm.shape[1] == 2560 and w_tm.shape[1] == 776

    def blockdiag(blocks):
        r = sum(b.shape[0] for b in blocks)
        c = sum(b.shape[1] for b in blocks)
        m = np.zeros((r, c), np.float32)
        i = j = 0
        for b in blocks:
            m[i:i + b.shape[0], j:j + b.shape[1]] = b
            i += b.shape[0]
            j += b.shape[1]
        return m

    wa = blockdiag(list(g('lru_w_a')))
    wx = blockdiag(list(g('lru_w_x')))
    bre = blockdiag([m.T for m in g('s5_b_re')])
    bim = blockdiag([m.T for m in g('s5_b_im')])
    cre = blockdiag([m.T for m in g('s5_c_re')])
    cim = blockdiag([m.T for m in g('s5_c_im')])
    sk = g('peer_subkeys')
    skbd = np.stack([blockdiag([sk[h, 0].T, sk[h, 1].T]) for h in range(8)], 0)
    return dict(pvec=pv, prow=pr, w_fm=np.ascontiguousarray(w_fm), w_tm=np.ascontiguousarray(w_tm),
                lru_wa=wa, lru_wx=wx, s5_bre=bre, s5_bim=bim, s5_cre=cre, s5_cim=cim,
                s5_glu_w=g('s5_glu_w'), w_out=g('w_out'), peer_wq=g('peer_wq'), peer_sk=skbd,
                peer_uv=np.concatenate([g('peer_u'), g('peer_v')], axis=1), ple_wg=g('ple_wg'), ple_wp=g('ple_wp'))


SUMM = dict(ret=128, lru=4, s5=16, gdn=256)
SOFF = dict(ret=0, lru=128, s5=132, gdn=148)
SW = 404


class Builder:
    def __init__(self, mode, stages=None, debug=()):
        self.mode = mode
        self.stages = stages
        self.debug = set(debug)
        nc = self.nc = bass.Bass("TRN2", target_bir_lowering=False)
        self.P = Prog(nc)
        self.din = {}
        self.dout = {}
        self.dscr = {}
        self.lp = ''
        NSB = 46000
        self.A = Arena(nc.alloc_sbuf_tensor("sb", [128, NSB], F32), "sb", NSB)
        self.AI = Arena(nc.alloc_sbuf_tensor("sbi", [128, 2600], I32), "sbi", 2600)
        pst = [nc.alloc_psum_tensor(f"ps{i}", [128, 512], F32) for i in range(8)]
        self.PS = [Buf(pst[i][:, :], "psum", 512 * i, 512 * i + 512) for i in range(8)]
        self.psi = 0
        self.nps = 8
        self.marks = []
        self.gdn_reuse = (mode in ('seq',))

    LAYER_INPUTS = ('pvec', 'prow', 'w_fm', 'w_tm', 'lru_wa', 'lru_wx', 's5_bre', 's5_bim', 's5_cre', 's5_cim',
                    's5_glu_w', 'w_out', 'peer_wq', 'peer_sk', 'peer_uv', 'ple_wg', 'ple_wp', 'pT')

    def inp(self, name, shape, dtype=F32):
        if name in self.LAYER_INPUTS:
            name = self.lp + name
        if name not in self.din:
            self.din[name] = self.nc.dram_tensor(name, list(shape), dtype, kind="ExternalInput").ap()
        return self.din[name]

    def outp(self, name, shape, dtype=F32):
        if name not in self.dout:
            self.dout[name] = self.nc.dram_tensor(name, list(shape), dtype, kind="ExternalOutput").ap()
        return self.dout[name]

    def scratch(self, name, shape, dtype=F32):
        if name in self.debug:
            return self.outp(name, shape, dtype)
        if name not in self.dscr:
            self.dscr[name] = self.nc.dram_tensor(name, list(shape), dtype).ap()
        return self.dscr[name]

    @staticmethod
    def Dm(ap, space, lo=0, hi=1):
        return Buf(ap, space, lo, hi)

    def ps(self):
        b = self.PS[self.psi % self.nps]
        self.psi += 1
        return b

    @staticmethod
    def _a(x):
        return x.ap if isinstance(x, Buf) else x

    @staticmethod
    def _bufs(*xs):
        return [x for x in xs if isinstance(x, Buf)]

    def dma(self, out, in_, eng='sp'):
        self.P.op(eng, lambda e, o=out.ap, i=in_.ap: e.dma_start(out=o, in_=i), reads=[in_], writes=[out], dma=True)

    def mm(self, out, lhsT, rhs, start=True, stop=True):
        self.P.op('pe', lambda e, o=out.ap, l=lhsT.ap, r=rhs.ap, s=start, t=stop: e.matmul(o, lhsT=l, rhs=r, start=s, stop=t),
                  reads=[lhsT, rhs], writes=[out])

    def tr(self, out, in_):
        idn = self.c('ident')
        n = in_.ap.shape[0]
        idv = idn.v(idn.ap[0:n, 0:n])
        self.P.op('pe', lambda e, o=out.ap, i=in_.ap, d=idv.ap: e.transpose(out=o, in_=i, identity=d), reads=[in_, idn], writes=[out])

    def tt(self, out, a, b, op, eng='dve'):
        self.P.op(eng, lambda e, o=out.ap, x=a.ap, y=b.ap: e.tensor_tensor(out=o, in0=x, in1=y, op=op), reads=[a, b], writes=[out])

    def ts(self, out, a, s1, op0, s2=None, op1=None, eng='dve', accum=None):
        kw = {}
        if op1 is not None:
            kw['op1'] = op1
        if accum is not None:
            kw['accum_out'] = accum.ap
        self.P.op(eng, lambda e, o=out.ap, x=a.ap, p=self._a(s1), q=self._a(s2): e.tensor_scalar(out=o, in0=x, scalar1=p, scalar2=q, op0=op0, **kw),
                  reads=[a] + self._bufs(s1, s2), writes=[out] + self._bufs(accum))

    def stt(self, out, a, s, b, op0, op1, eng='dve', accum=None):
        kw = {}
        if accum is not None:
            kw['accum_out'] = accum.ap
        self.P.op(eng, lambda e, o=out.ap, x=a.ap, p=self._a(s), y=b.ap: e.scalar_tensor_tensor(out=o, in0=x, scalar=p, in1=y, op0=op0, op1=op1, **kw),
                  reads=[a, b] + self._bufs(s), writes=[out] + self._bufs(accum))

    def act(self, out, in_, func, bias=None, scale=None, accum=None):
        kw = {}
        if bias is not None:
            kw['bias'] = self._a(bias)
        if scale is not None:
            kw['scale'] = self._a(scale)
        if accum is not None:
            kw['accum_out'] = accum.ap
        self.P.op('act', lambda e, o=out.ap, i=in_.ap: e.activation(out=o, in_=i, func=func, **kw),
                  reads=[in_] + self._bufs(bias, scale), writes=[out] + self._bufs(accum))

    def cp(self, out, in_, eng='dve'):
        if eng == 'act':
            self.P.op('act', lambda e, o=out.ap, i=in_.ap: e.copy(out=o, in_=i), reads=[in_], writes=[out])
        else:
            self.P.op(eng, lambda e, o=out.ap, i=in_.ap: e.tensor_copy(out=o, in_=i), reads=[in_], writes=[out])

    def red(self, out, in_, op=ALU.add, eng='dve'):
        self.P.op(eng, lambda e, o=out.ap, i=in_.ap: e.tensor_reduce(out=o, in_=i, axis=AX.X, op=op), reads=[in_], writes=[out])

    def memset(self, out, val, eng='dve'):
        self.P.op(eng, lambda e, o=out.ap: e.memset(o, val), writes=[out])

    def recip(self, out, in_):
        self.P.op('dve', lambda e, o=out.ap, i=in_.ap: e.reciprocal(out=o, in_=i), reads=[in_], writes=[out])

    def scan(self, out, d0, d1, init):
        self.P.op('dve', lambda e, o=out.ap, a=d0.ap, b=d1.ap, i=self._a(init): e.tensor_tensor_scan(out=o, data0=a, data1=b, initial=i, op0=ALU.mult, op1=ALU.add),
                  reads=[d0, d1] + self._bufs(init), writes=[out])

    def rsqrt(self, out, in_, scale=1.0, eps=EPS):
        self.act(out, in_, AF.Sqrt, bias=self.epsc if eps == EPS else eps, scale=scale)
        self.recip(out, out)

    def c(self, name):
        a, b = CO[name]
        return self.consts.sub(a, b)

    def pv(self, name, j=None, n=1):
        a, b = PV[name]
        if j is None:
            return self.pvec.sub(a, b)
        return self.pvec.sub(a + j, a + j + n)

    def pr(self, name):
        a, b = PR[name]
        return self.prow.sub(a, b)

    def sincos(self, sin_out, cos_out, ang, tmpk, tmpa):
        for (dst, shift) in ((sin_out, 0.0), (cos_out, math.pi / 2)):
            if dst is None:
                continue
            src = ang
            if shift != 0.0:
                self.ts(tmpa, ang, shift, ALU.add)
                src = tmpa
            self.ts(tmpk, src, 1.0 / TWO_PI, ALU.mult, MAGIC, ALU.add)
            self.ts(tmpk, tmpk, MAGIC, ALU.subtract)
            self.stt(tmpa, tmpk, -CW1, src, ALU.mult, ALU.add)
            self.stt(tmpa, tmpk, -CW2, tmpa, ALU.mult, ALU.add)
            self.ts(tmpa, tmpa, -PI_SAFE, ALU.max, PI_SAFE, ALU.min)
            self.act(dst, tmpa, AF.Sin)

    def load_layer_params(self):
        self.dma(self.pvec, self.Dm(self.inp("pvec", [128, NPV]), self.lp + "pvec"))
        self.dma(self.prow, self.Dm(self.inp("prow", [128, NPR]), self.lp + "prow"))

    def mark(self, label):
        self.marks.append((label, self.P.cnt['dve']))

    def mixers(self, full):
        st = self.stages
        on = lambda s_: (st is None or s_ in st)
        tag = 'AB' if full else 'A'
        if on('ret'):
            self.mark('ret' + tag)
            self.stage_ret(full)
        if on('lru'):
            self.mark('lru' + tag)
            self.stage_lru(full)
        if on('s5'):
            self.mark('s5' + tag)
            self.stage_s5(full)
        if on('gdn'):
            self.mark('gdn' + tag)
            self.stage_gdn(full)
        self.mark('end' + tag)

    def allgather(self, src, src_space, dst, dst_space):
        self.P.op('pool', lambda e, i=src, o=dst: e.collective_compute(
            "AllGather", ALU.bypass, replica_groups=[list(range(NCORES))], ins=[i], outs=[o]),
            reads=[self.Dm(src, src_space, 0, 1 << 40)], writes=[self.Dm(dst, dst_space, 0, 1 << 40)], cc=True)

    def build(self):
        A = self.A
        mode = self.mode
        st = self.stages
        on = lambda s_: (st is None or s_ in st)
        self.consts = A.alloc(NCONST)
        self.pvec = A.alloc(NPV)
        self.prow = A.alloc(NPR)
        self.epsc = A.alloc(1)
        self.cmask = A.alloc(16)
        self.dma(self.consts, self.Dm(self.inp("consts", [128, NCONST]), "consts"))
        self.memset(self.epsc, EPS)
        if mode != 'A':
            self.dma(self.cmask, self.Dm(self.inp("cmask", [128, 16]), "cmask"))
        if mode == 'fused':
            for l_ in range(2):
                self.scratch("summ%d" % l_, [128, SW])
                self.scratch("gath%d" % l_, [NCORES * 128, SW])
            self.scratch("tail", [128, 8 * HALO])
            self.scratch("gtail", [NCORES * 128, 8 * HALO])
        self.fm = self.scratch("fm", [2560, TH])
        self.tm = self.scratch("tm", [T, 776])
        self.mixed = self.scratch("mixed", [D, T])
        self.hx = self.inp("hx", [D, TH])
        self.hx_space = "hx"
        if mode == 'A':
            self.load_layer_params()
            self.summ = self.outp("summ", [128, SW])
            if on('proj'):
                self.stage_proj()
            self.mixers(False)
        elif mode == 'AB':
            self.load_layer_params()
            self.gath = self.inp("gath", [NCORES * 128, SW])
            if on('proj'):
                self.stage_proj()
            self.mixers(True)
            if on('post'):
                self.stage_post('both')
        elif mode == 'seq':
            self.load_layer_params()
            self.summ = self.scratch("summ", [128, SW])
            self.gath = self.inp("gath", [NCORES * 128, SW])
            self.stage_proj()
            self.mixers(False)
            self.mixers(True)
            if on('post'):
                self.stage_post('both')
        else:
            for l in range(2):
                self.lp = "l%d_" % l
                self.load_layer_params()
                self.summ = self.scratch("summ%d" % l, [128, SW])
                self.gath = self.scratch("gath%d" % l, [NCORES * 128, SW])
                self.mark('proj')
                self.stage_proj()
                self.mixers(False)
                self.allgather(self.summ, "summ", self.gath, "gath")
                self.mixers(True)
                self.mark('post')
                if l == 0:
                    self.stage_post('next')
                else:
                    self.stage_post('final')
        self.P.final_wait_all('sp')
        self.P.emit()
        return self.nc

    def rms_fm(self, src_fn, ntile, ncols, gname, dst_fn, dimscale):
        A = self.A
        m = A.mark()
        sq = A.alloc(ncols)
        rstd = A.alloc(ncols)
        ps = self.ps().sub(0, ncols)
        for j in range(ntile):
            s = src_fn(j)
            self.tt(sq, s, s, ALU.mult, eng='pool' if j % 2 else 'dve')
            self.mm(ps, self.c('ones'), sq, start=(j == 0), stop=(j == ntile - 1))
        self.rsqrt(rstd, ps, scale=dimscale)
        for j in range(ntile):
            self.stt(dst_fn(j), src_fn(j), self.pv(gname, j), rstd, ALU.mult, ALU.mult)
        A.release(m)

    def stage_proj(self):
        A = self.A
        hx = self.hx
        m0 = A.mark()
        zT = A.alloc(8 * TH)
        z3 = zT.r3(8)
        hxv = hx.rearrange("(j p) t -> p j t", p=128)
        blocks = [(0, HALO)] + [(HALO + b * 512, 512) for b in range(NB)]
        mh_ = A.mark()
        hb = [A.alloc(8 * 512), A.alloc(8 * 512)]
        for bi, (c0, n) in enumerate(blocks):
            h = hb[bi % 2]
            h3 = h.v(h.ap[:, 0:8 * n].rearrange("p (a b) -> p a b", a=8))
            self.dma(h3, self.Dm(hxv[:, :, c0:c0 + n], self.hx_space))
            self.rms_fm(lambda j: h.v(h.ap[:, j * n:(j + 1) * n]), 8, n, 'mix_norm',
                        lambda j: zT.v(z3.ap[:, j, c0:c0 + n]), 1.0 / D)
        if 'zT' in self.debug:
            zo = self.outp("zT", [D, TH])
            self.dma(self.Dm(zo.rearrange("(j p) t -> p j t", p=128), "zTo"), z3)
        A.release(mh_)
        wfm = self.inp("w_fm", [D, 2560]).rearrange("(j p) n -> p j n", p=128)
        wb = [A.alloc(8 * 512), A.alloc(8 * 512)]
        stg = [A.alloc(TH), A.alloc(TH)]
        si = 0
        for cg in range(5):
            w = wb[cg % 2]
            w3 = w.r3(8)
            self.dma(w3, self.Dm(wfm[:, :, cg * 512:(cg + 1) * 512], "w_fm"))
            for n4 in range(4):
                row0 = cg * 512 + n4 * 128
                halo = row0 >= 1280 and not (1536 <= row0 < 1792)
                s = stg[si % 2]
                si += 1
                for bi, (c0, n) in enumerate(blocks):
                    if bi == 0 and not halo:
                        continue
                    ps = self.ps().sub(0, n)
                    for kc in range(8):
                        self.mm(ps, w.v(w3.ap[:, kc, n4 * 128:(n4 + 1) * 128]), zT.v(z3.ap[:, kc, c0:c0 + n]),
                                start=(kc == 0), stop=(kc == 7))
                    self.cp(s.sub(c0, c0 + n), ps, eng='act' if bi % 2 else 'dve')
                lo = 0 if halo else HALO
                self.dma(self.Dm(self.fm[row0:row0 + 128, lo:TH], "fm", row0, row0 + 128), s.sub(lo, TH))
        A.release(mh_)
        wt = A.alloc(8 * 776)
        wt3 = wt.r3(8)
        self.dma(wt3, self.Dm(self.inp("w_tm", [D, 776]).rearrange("(j p) n -> p j n", p=128), "w_tm"))
        st2 = [A.alloc(776), A.alloc(776)]
        for tt_ in range(NT):
            c0 = HALO + tt_ * 128
            s = st2[tt_ % 2]
            for (a, b) in ((0, 512), (512, 776)):
                ps = self.ps().sub(0, b - a)
                for kc in range(8):
                    self.mm(ps, zT.v(z3.ap[:, kc, c0:c0 + 128]), wt.v(wt3.ap[:, kc, a:b]), start=(kc == 0), stop=(kc == 7))
                self.cp(s.sub(a, b), ps, eng='act' if a else 'dve')
            self.dma(self.Dm(self.tm[tt_ * 128:(tt_ + 1) * 128, :], "tm", tt_ * 128, tt_ * 128 + 128), s)
        A.release(m0)

    def load_slots(self, name):
        w = SUMM[name]
        off = SOFF[name]
        sl = self.A.alloc(7 * w)
        src = self.gath.rearrange("(r p) n -> p r n", p=128)[:, 0:7, off:off + w]
        self.dma(sl.r3(7), self.Dm(src, "gath"))
        return sl

    def blend(self, state, new, k):
        self.tt(new, new, state, ALU.subtract)
        self.stt(state, new, self.cmask.sub(k, k + 1), state, ALU.mult, ALU.add)

    def so_dst(self, name):
        off = SOFF[name]
        return self.Dm(self.summ[:, off:off + SUMM[name]], "summ", off, off + SUMM[name])

    def store_mixed_tm(self, y, ytile, base_row, stgfm):
        for j in range(2):
            ps = self.ps().sub(0, 128)
            self.tr(ps, y.sub(j * 128, (j + 1) * 128))
            self.cp(stgfm.sub(j * T + ytile * 128, j * T + ytile * 128 + 128), ps, eng='act')

    def branch_norm_tm(self, y, bn_name, tmp, col):
        self.memset(col, 0.0)
        self.stt(tmp, y, 1.0, y, ALU.mult, ALU.mult, accum=col)
        self.rsqrt(col, col, scale=1.0 / 256)
        self.stt(y, y, col, self.pr(bn_name), ALU.mult, ALU.mult)

    def stage_ret(self, full):
        A = self.A
        m0 = A.mark()
        qk = A.alloc(4 * T)
        S = A.alloc(128)
        mt_ = A.mark()
        pos_i = self.AI.alloc(T)
        pos_d = self.inp("pos", [1, T], I32)
        self.dma(pos_i, self.Dm(pos_d.to_broadcast([128, T]), "pos"))
        ang = A.alloc(T); tk = A.alloc(T); ta = A.alloc(T)
        cosT = A.alloc(T); sinT = A.alloc(T)
        self.cp(ang, pos_i)
        self.ts(ang, ang, self.c('invf'), ALU.mult)
        self.sincos(sinT, cosT, ang, tk, ta)
        self.ts(sinT, sinT, self.c('sgn'), ALU.mult)
        for i in range(4):
            base = (0 if i < 2 else 512) + (i % 2) * 128
            self.dma(tk, self.Dm(self.fm[base:base + 128, HALO:TH], "fm", base, base + 128))
            self.dma(ta, self.Dm(self.fm[base + 256:base + 384, HALO:TH], "fm", base + 256, base + 384))
            dst = qk.sub(i * T, (i + 1) * T)
            self.tt(tk, tk, cosT, ALU.mult)
            self.tt(ta, ta, sinT, ALU.mult, eng='pool')
            self.tt(dst, tk, ta, ALU.add)
            if i < 2:
                self.ts(dst, dst, 0.125, ALU.mult)
        A.release(mt_)
        self.AI.release(0)
        if full:
            mc_ = A.mark()
            sl = self.load_slots('ret')
            tmpS = A.alloc(128)
            self.memset(S, 0.0)
            for k_ in range(7):
                self.tt(tmpS, S, self.c('cdT'), ALU.mult)
                self.tt(tmpS, tmpS, sl.sub(k_ * 128, (k_ + 1) * 128), ALU.add)
                self.blend(S, tmpS, k_)
            A.release(mc_)
        else:
            self.memset(S, 0.0)
        stgfm = A.alloc(2 * T) if full else None
        vg = [A.alloc(512), A.alloc(512)]
        for c in range(NT):
            cs = slice(c * 128, (c + 1) * 128)
            v = vg[c % 2]
            self.dma(v, self.Dm(self.tm[c * 128:(c + 1) * 128, 0:512], "tm", c * 128, c * 128 + 128))
            m1 = A.mark()
            kdec = A.alloc(256)
            pk = self.ps()
            for hp in range(2):
                self.tr(pk.sub(hp * 128, hp * 128 + 128), qk.sub((2 + hp) * T + c * 128, (2 + hp) * T + c * 128 + 128))
            self.tt(kdec, pk.sub(0, 256), self.c('kdec'), ALU.mult)
            if full:
                PT = A.alloc(512)
                qd = A.alloc(512)
                kz = A.alloc(512)
                psc = self.ps()
                for h in range(4):
                    hp, par = h // 2, h % 2
                    kt = qk.sub((2 + hp) * T + c * 128, (2 + hp) * T + c * 128 + 128)
                    qt = qk.sub(hp * T + c * 128, hp * T + c * 128 + 128)
                    hs = (h * 128, h * 128 + 128)
                    self.ts(kz.sub(*hs), kt, self.c('blk64').sub(par * 64, par * 64 + 1), ALU.mult, eng='pool' if par else 'dve')
                    self.tt(qd.sub(*hs), qt, self.c('qdec').sub(*hs), ALU.mult, eng='pool' if par else 'dve')
                    self.mm(psc.sub(*hs), kz.sub(*hs), qt)
                self.tt(PT, psc, self.c('maskT'), ALU.mult)
                po = self.ps()
                for h in range(4):
                    hp, par = h // 2, h % 2
                    o = po.sub(h * 64, h * 64 + 64)
                    self.mm(o, PT.sub(h * 128, h * 128 + 128), v.sub(h * 64, h * 64 + 64), start=True, stop=False)
                    self.mm(o, qd.sub(h * 128, h * 128 + 128), S.sub(hp * 64, hp * 64 + 64), start=False, stop=True)
                o = A.alloc(256); o2 = A.alloc(256); st4 = A.alloc(4); col = A.alloc(1)
                self.cp(o, po.sub(0, 256), eng='act')
                o3 = o.r3(4)
                if DBG_STOP & 1:
                    self.store_mixed_tm(o, c, 0, stgfm)
                    A.release(m1)
                    continue
                self.red(st4, o3)
                self.ts(st4, st4, 1.0 / 64, ALU.mult)
                st4b = st4.v(st4.ap.unsqueeze(2).to_broadcast([128, 4, 64]))
                self.tt(o3, o3, st4b, ALU.subtract)
                self.tt(o2, o, o, ALU.mult, eng='pool')
                self.red(st4, o2.r3(4))
                self.rsqrt(st4, st4, scale=1.0 / 64)
                self.tt(o3, o3, st4b, ALU.mult)
                if DBG_STOP & 2:
                    self.store_mixed_tm(o, c, 0, stgfm)
                    A.release(m1)
                    continue
                self.tt(o, o, self.pr('ret_gn'), ALU.mult)
                self.act(o2, v.sub(256, 512), AF.Silu)
                self.tt(o, o, o2, ALU.mult)
                self.branch_norm_tm(o, 'bn0', o2, col)
                self.store_mixed_tm(o, c, 0, stgfm)
            pS = self.ps()
            for h in range(4):
                hp, par = h // 2, h % 2
                self.mm(pS.v(pS.ap[par * 64:(par + 1) * 64, hp * 64:hp * 64 + 64]), kdec.sub(h * 64, h * 64 + 64), v.sub(h * 64, h * 64 + 64))
            self.tt(S, S, self.c('cd'), ALU.mult)
            self.tt(S, S, pS.sub(0, 128), ALU.add)
            A.release(m1)
        if full:
            self.dma(self.Dm(self.mixed[0:256, :].rearrange("(j p) t -> p j t", p=128), "mixed", 0, 256), stgfm.r3(2))
        else:
            self.dma(self.so_dst('ret'), S)
        A.release(m0)

    def conv4(self, dst, xin, wname, tile):
        a, _ = PV[wname]
        wc = lambda j: self.pvec.sub(a + tile * 4 + j, a + tile * 4 + j + 1)
        self.ts(dst, xin.sub(0, T), wc(0), ALU.mult)
        for j in range(1, 4):
            self.stt(dst, xin.sub(j, j + T), wc(j), dst, ALU.mult, ALU.add)

    def stage_lru(self, full):
        A = self.A
        m0 = A.mark()
        wa = A.alloc(512); wx = A.alloc(512)
        self.dma(wa.r3(2), self.Dm(self.inp("lru_wa", [256, 256]).rearrange("(j p) n -> p j n", p=128), "lru_wa"))
        self.dma(wx.r3(2), self.Dm(self.inp("lru_wx", [256, 256]).rearrange("(j p) n -> p j n", p=128), "lru_wx"))
        sc = A.alloc(2)
        self.act(sc, self.pv('lru_lam'), AF.Exp, scale=-1.0)
        self.act(sc, sc, AF.Ln, bias=1.0)
        self.ts(sc, sc, -8.0, ALU.mult)
        xb = A.alloc(2 * T)
        xin = A.alloc(TH)
        for j in range(2):
            self.dma(xin, self.Dm(self.fm[1280 + j * 128:1408 + j * 128, :], "fm", 1280 + j * 128, 1408 + j * 128))
            d = xb.sub(j * T, (j + 1) * T)
            self.conv4(d, xin, 'lru_cw', j)
            self.ts(d, d, self.pv('lru_cb', j), ALU.add)
        a_ = A.alloc(2 * T); b_ = A.alloc(2 * T)
        for j in range(2):
            for blk in range(NB):
                cs = (j * T + blk * 512, j * T + blk * 512 + 512)
                pa = self.ps(); pi = self.ps()
                for kc in range(2):
                    rhs = xb.sub(kc * T + blk * 512, kc * T + blk * 512 + 512)
                    self.mm(pa, wa.sub(kc * 256 + j * 128, kc * 256 + j * 128 + 128), rhs, start=(kc == 0), stop=(kc == 1))
                    self.mm(pi, wx.sub(kc * 256 + j * 128, kc * 256 + j * 128 + 128), rhs, start=(kc == 0), stop=(kc == 1))
                aa = a_.sub(*cs); bb = b_.sub(*cs)
                self.act(aa, pa, AF.Sigmoid, bias=self.pv('lru_ba', j))
                self.act(bb, pi, AF.Sigmoid, bias=self.pv('lru_bx', j))
                self.tt(bb, bb, xb.sub(*cs), ALU.mult)
        logA = A.alloc(2)
        for j in range(2):
            aj = a_.sub(j * T, (j + 1) * T); bj = b_.sub(j * T, (j + 1) * T)
            self.ts(aj, aj, sc.sub(j, j + 1), ALU.mult)
            self.red(logA.sub(j, j + 1), aj)
            self.act(aj, aj, AF.Exp)
            m1 = A.mark()
            t1 = A.alloc(T)
            self.tt(t1, aj, aj, ALU.mult, eng='pool')
            self.ts(t1, t1, -1.0, ALU.mult, 1.0, ALU.add)
            self.ts(t1, t1, 0.0, ALU.max)
            self.act(t1, t1, AF.Sqrt)
            self.tt(bj, bj, t1, ALU.mult)
            A.release(m1)
        h0 = A.alloc(2)
        if full:
            sl = self.load_slots('lru')
            self.memset(h0, 0.0)
            ea = A.alloc(2); tn = A.alloc(2)
            for s_ in range(7):
                self.act(ea, sl.sub(s_ * 4 + 2, s_ * 4 + 4), AF.Exp)
                self.tt(tn, h0, ea, ALU.mult)
                self.tt(tn, tn, sl.sub(s_ * 4, s_ * 4 + 2), ALU.add)
                self.blend(h0, tn, s_)
        else:
            self.memset(h0, 0.0)
        hh = xb
        for j in range(2):
            self.scan(hh.sub(j * T, (j + 1) * T), a_.sub(j * T, (j + 1) * T), b_.sub(j * T, (j + 1) * T), h0.sub(j, j + 1))
        if not full:
            so = A.alloc(4)
            for j in range(2):
                self.cp(so.sub(j, j + 1), hh.sub(j * T + T - 1, j * T + T))
            self.cp(so.sub(2, 4), logA)
            self.dma(self.so_dst('lru'), so)
        else:
            g = a_
            for j in range(2):
                self.dma(g.sub(j * T, (j + 1) * T), self.Dm(self.fm[1024 + j * 128:1152 + j * 128, HALO:TH], "fm", 1024 + j * 128, 1152 + j * 128))
                self.act(g.sub(j * T, (j + 1) * T), g.sub(j * T, (j + 1) * T), AF.Gelu)
                self.tt(hh.sub(j * T, (j + 1) * T), hh.sub(j * T, (j + 1) * T), g.sub(j * T, (j + 1) * T), ALU.mult)
            for blk in range(NB):
                self.rms_fm(lambda j: hh.sub(j * T + blk * 512, j * T + blk * 512 + 512), 2, 512, 'bn1',
                            lambda j: b_.sub(j * T + blk * 512, j * T + blk * 512 + 512), 1.0 / 256)
            self.dma(self.Dm(self.mixed[256:512, :].rearrange("(j p) t -> p j t", p=128), "mixed", 256, 512), b_.r3(2))
        A.release(m0)

    def stage_s5(self, full):
        A = self.A
        m0 = A.mark()
        L = 256
        NBL = T // L
        are = self.pv('s5_are'); aim = self.pv('s5_aim')
        dt = A.alloc(8); rho = A.alloc(8); th = A.alloc(8)
        self.act(dt, self.pv('s5_ldt'), AF.Exp)
        self.tt(rho, are, dt, ALU.mult)
        self.act(rho, rho, AF.Exp)
        self.tt(th, aim, dt, ALU.mult)
        tk8 = A.alloc(8); ta8 = A.alloc(8); cs1 = A.alloc(8); sn1 = A.alloc(8)
        self.sincos(sn1, cs1, th, tk8, ta8)
        abre = A.alloc(8); abim = A.alloc(8)
        self.tt(abre, rho, cs1, ALU.mult)
        self.tt(abim, rho, sn1, ALU.mult)
        den = A.alloc(8); pre = A.alloc(8); fre = A.alloc(8); fim = A.alloc(8); t8 = A.alloc(8)
        self.tt(den, are, are, ALU.mult)
        self.tt(t8, aim, aim, ALU.mult)
        self.tt(den, den, t8, ALU.add)
        self.recip(den, den)
        self.ts(pre, abre, -1.0, ALU.add)
        self.tt(fre, pre, are, ALU.mult)
        self.tt(t8, abim, aim, ALU.mult)
        self.tt(fre, fre, t8, ALU.add)
        self.tt(fre, fre, den, ALU.mult)
        self.tt(fim, abim, are, ALU.mult)
        self.tt(t8, pre, aim, ALU.mult)
        self.tt(fim, fim, t8, ALU.subtract)
        self.tt(fim, fim, den, ALU.mult)
        ctab = A.alloc(8 * (L + 4)); stab = A.alloc(8 * (L + 4))
        LW = L + 4
        tkk = A.alloc(LW); taa = A.alloc(LW); an = A.alloc(LW)
        tau = self.c('tau').sub(0, LW)
        for j in range(8):
            self.ts(an, tau, th.sub(j, j + 1), ALU.mult)
            self.sincos(stab.sub(j * LW, (j + 1) * LW), ctab.sub(j * LW, (j + 1) * LW), an, tkk, taa)
        Er = A.alloc(8 * L); Ei = A.alloc(8 * L)
        for j in range(8):
            cj = ctab.sub(j * LW, j * LW + L); sj = stab.sub(j * LW, j * LW + L)
            er = Er.sub(j * L, (j + 1) * L); ei = Ei.sub(j * L, (j + 1) * L)
            self.ts(er, cj, fre.sub(j, j + 1), ALU.mult)
            self.stt(er, sj, fim.sub(j, j + 1), er, ALU.mult, ALU.add)
            self.ts(ei, sj, fre.sub(j, j + 1), ALU.mult, -1.0, ALU.mult)
            self.stt(ei, cj, fim.sub(j, j + 1), ei, ALU.mult, ALU.add)
        xre = A.alloc(8); xim = A.alloc(8)
        if full:
            rL = A.alloc(8); aLr = A.alloc(8); aLi = A.alloc(8); t8b = A.alloc(8)
            self.tt(rL, are, dt, ALU.mult)
            self.act(rL, rL, AF.Exp, scale=float(L))
            for j in range(8):
                self.tt(aLr.sub(j, j + 1), rL.sub(j, j + 1), ctab.sub(j * LW + L, j * LW + L + 1), ALU.mult)
                self.tt(aLi.sub(j, j + 1), rL.sub(j, j + 1), stab.sub(j * LW + L, j * LW + L + 1), ALU.mult)
            nsq = int(round(math.log2(T // L)))
            for _ in range(nsq):
                self.tt(t8, aLr, aLi, ALU.mult)
                self.tt(aLr, aLr, aLr, ALU.mult)
                self.tt(t8b, aLi, aLi, ALU.mult)
                self.tt(aLr, aLr, t8b, ALU.subtract)
                self.ts(aLi, t8, 2.0, ALU.mult)
            sl = self.load_slots('s5')
            self.memset(xre, 0.0); self.memset(xim, 0.0)
            nre = A.alloc(8); nim = A.alloc(8)
            for s_ in range(7):
                self.tt(nre, xre, aLr, ALU.mult)
                self.tt(t8b, xim, aLi, ALU.mult)
                self.tt(nre, nre, t8b, ALU.subtract)
                self.tt(nre, nre, sl.sub(s_ * 16, s_ * 16 + 8), ALU.add)
                self.tt(nim, xre, aLi, ALU.mult)
                self.tt(t8b, xim, aLr, ALU.mult)
                self.tt(nim, nim, t8b, ALU.add)
                self.tt(nim, nim, sl.sub(s_ * 16 + 8, s_ * 16 + 16), ALU.add)
                self.blend(xre, nre, s_)
                self.blend(xim, nim, s_)
        else:
            self.memset(xre, 0.0); self.memset(xim, 0.0)
        bre = A.alloc(2 * 1024); bim = A.alloc(2 * 1024)
        self.dma(bre.r3(2), self.Dm(self.inp("s5_bre", [256, 1024]).rearrange("(j p) n -> p j n", p=128), "s5_bre"))
        self.dma(bim.r3(2), self.Dm(self.inp("s5_bim", [256, 1024]).rearrange("(j p) n -> p j n", p=128), "s5_bim"))
        if full:
            cre = A.alloc(8 * 256); cim = A.alloc(8 * 256); glu = A.alloc(2 * 512)
            self.dma(cre.r3(8), self.Dm(self.inp("s5_cre", [1024, 256]).rearrange("(j p) n -> p j n", p=128), "s5_cre"))
            self.dma(cim.r3(8), self.Dm(self.inp("s5_cim", [1024, 256]).rearrange("(j p) n -> p j n", p=128), "s5_cim"))
            self.dma(glu.r3(2), self.Dm(self.inp("s5_glu_w", [256, 512]).rearrange("(j p) n -> p j n", p=128), "s5_glu_w"))
            self.ts(cim, cim, -1.0, ALU.mult)
            outb = A.alloc(2 * T)
        w0r = A.alloc(8); w0i = A.alloc(8)
        u = [A.alloc(2 * L), A.alloc(2 * L)]
        for blk in range(NBL):
            ub = u[blk % 2]
            self.dma(ub.r3(2), self.Dm(self.fm[1536:1792, HALO + blk * L:HALO + (blk + 1) * L].rearrange("(j p) t -> p j t", p=128), "fm", 1536, 1792))
            self.tt(w0r, xre, cs1, ALU.mult); self.tt(t8, xim, sn1, ALU.mult); self.tt(w0r, w0r, t8, ALU.subtract)
            self.tt(w0i, xre, sn1, ALU.mult); self.tt(t8, xim, cs1, ALU.mult); self.tt(w0i, w0i, t8, ALU.add)
            m1 = A.mark()
            if full:
                xr_all = A.alloc(8 * L); xi_all = A.alloc(8 * L)
            for j in range(8):
                kc = j // 4
                pr_ = self.ps().sub(0, L); pi_ = self.ps().sub(0, L)
                self.mm(pr_, bre.sub(kc * 1024 + j * 128, kc * 1024 + j * 128 + 128), ub.sub(kc * L, (kc + 1) * L))
                self.mm(pi_, bim.sub(kc * 1024 + j * 128, kc * 1024 + j * 128 + 128), ub.sub(kc * L, (kc + 1) * L))
                m2 = A.mark()
                wr = A.alloc(L); wi = A.alloc(L); t1 = A.alloc(L); t2 = A.alloc(L)
                er = Er.sub(j * L, (j + 1) * L); ei = Ei.sub(j * L, (j + 1) * L)
                self.tt(wr, pr_, er, ALU.mult)
                self.tt(t1, pi_, ei, ALU.mult)
                self.tt(wr, wr, t1, ALU.subtract, eng='pool')
                self.tt(wi, pi_, er, ALU.mult)
                self.tt(t2, pr_, ei, ALU.mult)
                self.tt(wi, wi, t2, ALU.add, eng='pool')
                rb = rho.v(rho.ap[:, j:j + 1].to_broadcast([128, L]))
                self.scan(wr, rb, wr, w0r.sub(j, j + 1))
                self.scan(wi, rb, wi, w0i.sub(j, j + 1))
                cj = ctab.sub(j * LW, j * LW + L); sj = stab.sub(j * LW, j * LW + L)
                if full:
                    xr = xr_all.sub(j * L, (j + 1) * L); xi = xi_all.sub(j * L, (j + 1) * L)
                    self.tt(xr, wr, cj, ALU.mult); self.tt(t1, wi, sj, ALU.mult, eng='pool'); self.tt(xr, xr, t1, ALU.subtract)
                    self.tt(xi, wr, sj, ALU.mult); self.tt(t2, wi, cj, ALU.mult, eng='pool'); self.tt(xi, xi, t2, ALU.add)
                    self.cp(xre.sub(j, j + 1), xr.sub(L - 1, L)); self.cp(xim.sub(j, j + 1), xi.sub(L - 1, L))
                else:
                    c1 = cj.sub(L - 1, L); s1 = sj.sub(L - 1, L); a1 = wr.sub(L - 1, L); b1 = wi.sub(L - 1, L)
                    self.tt(xre.sub(j, j + 1), a1, c1, ALU.mult); self.tt(t1.sub(0, 1), b1, s1, ALU.mult)
                    self.tt(xre.sub(j, j + 1), xre.sub(j, j + 1), t1.sub(0, 1), ALU.subtract)
                    self.tt(xim.sub(j, j + 1), a1, s1, ALU.mult); self.tt(t1.sub(0, 1), b1, c1, ALU.mult)
                    self.tt(xim.sub(j, j + 1), xim.sub(j, j + 1), t1.sub(0, 1), ALU.add)
                A.release(m2)
            if full:
                yb = A.alloc(2 * L)
                for mt in range(2):
                    py = self.ps().sub(0, L)
                    for q in range(4):
                        j = mt * 4 + q
                        self.mm(py, cre.sub(j * 256 + mt * 128, j * 256 + mt * 128 + 128), xr_all.sub(j * L, (j + 1) * L), start=(q == 0), stop=False)
                        self.mm(py, cim.sub(j * 256 + mt * 128, j * 256 + mt * 128 + 128), xi_all.sub(j * L, (j + 1) * L), start=False, stop=(q == 3))
                    y = yb.sub(mt * L, (mt + 1) * L)
                    self.stt(y, ub.sub(mt * L, (mt + 1) * L), self.pv('s5_d', mt), py, ALU.mult, ALU.add)
                    self.act(y, y, AF.Gelu)
                zz = A.alloc(4 * L)
                for nt in range(4):
                    pz = self.ps().sub(0, L)
                    for kc in range(2):
                        self.mm(pz, glu.sub(kc * 512 + nt * 128, kc * 512 + nt * 128 + 128), yb.sub(kc * L, (kc + 1) * L), start=(kc == 0), stop=(kc == 1))
                    if nt < 2:
                        self.ts(zz.sub(nt * L, (nt + 1) * L), pz, self.pv('s5_glub', nt), ALU.add)
                    else:
                        self.act(zz.sub(nt * L, (nt + 1) * L), pz, AF.Sigmoid, bias=self.pv('s5_glub', nt))
                for mt in range(2):
                    self.tt(zz.sub(mt * L, (mt + 1) * L), zz.sub(mt * L, (mt + 1) * L), zz.sub((mt + 2) * L, (mt + 3) * L), ALU.mult)
                self.rms_fm(lambda j: zz.sub(j * L, (j + 1) * L), 2, L, 'bn2',
                            lambda j: outb.sub(j * T + blk * L, j * T + (blk + 1) * L), 1.0 / 256)
            A.release(m1)
        if full:
            self.dma(self.Dm(self.mixed[512:768, :].rearrange("(j p) t -> p j t", p=128), "mixed", 512, 768), outb.r3(2))
        else:
            so = A.alloc(16)
            self.cp(so.sub(0, 8), xre); self.cp(so.sub(8, 16), xim)
            self.dma(self.so_dst('s5'), so)
        A.release(m0)

    PK = 2312

    def stage_gdn(self, full):
        A = self.A
        m0 = A.mark()
        W = 64 if full else 128
        reuse = full and self.gdn_reuse
        gprep = self.scratch("gprep", [NT * 128, self.PK])
        cv = None
        if not reuse:
            cv = A.alloc(6 * T)
            mx_ = A.mark()
            xin = [A.alloc(TH), A.alloc(TH)]
            for i in range(6):
                x = xin[i % 2]
                self.dma(x, self.Dm(self.fm[1792 + i * 128:1920 + i * 128, :], "fm", 1792 + i * 128, 1920 + i * 128))
                d = cv.sub(i * T, (i + 1) * T)
                self.conv4(d, x, 'gdn_cw', i)
                self.act(d, d, AF.Silu)
            A.release(mx_)
            for i in range(4):
                for blk in range(NB):
                    d = cv.sub(i * T + blk * 512, i * T + blk * 512 + 512)
                    m1 = A.mark()
                    sq = A.alloc(512)
                    self.tt(sq, d, d, ALU.mult, eng='pool')
                    ps = self.ps()
                    self.mm(ps, self.c('blk64'), sq)
                    self.rsqrt(sq, ps)
                    if i < 2:
                        self.stt(d, d, 0.125, sq, ALU.mult, ALU.mult)
                    else:
                        self.tt(d, d, sq, ALU.mult)
                    A.release(m1)
        S = A.alloc(2 * W)
        if full:
            self.gdn_combine(S)
        else:
            self.memset(S, 0.0)
            for hp in range(2):
                self.cp(S.sub(hp * W + 64, hp * W + 128), self.c('pinit'))
        stgfm = A.alloc(2 * T) if full else None
        tmb = [A.alloc(264), A.alloc(264)]
        pkb = [A.alloc(self.PK), A.alloc(self.PK)]
        for tl in range(NT):
            c0 = tl * 128
            m1 = A.mark()
            tmv = tmb[tl % 2]
            pack = pkb[tl % 2]
            self.dma(tmv, self.Dm(self.tm[c0:c0 + 128, 512:776], "tm", c0, c0 + 128))
            if reuse:
                self.dma(pack, self.Dm(gprep[c0:c0 + 128, :], "gprep", c0, c0 + 128))
            else:
                self.gdn_prep(tl, cv, tmv, pack)
                if self.gdn_reuse and not full:
                    self.dma(self.Dm(gprep[c0:c0 + 128, :], "gprep", c0, c0 + 128), pack)
            self.gdn_rec(tl, pack, tmv, S, W, full, stgfm)
            A.release(m1)
        if full:
            self.dma(self.Dm(self.mixed[768:1024, :].rearrange("(j p) t -> p j t", p=128), "mixed", 768, 1024), stgfm.r3(2))
        else:
            self.dma(self.so_dst('gdn'), S)
        A.release(m0)

    def gdn_prep(self, tl, cv, tmv, pack):
        A = self.A
        c0 = tl * 128
        uu = pack.sub(0, 256); wT = pack.sub(256, 768); kd = pack.sub(768, 1280)
        attnT = pack.sub(1280, 1792); qz = pack.sub(1792, 2304); eg = pack.sub(2304, 2308); cdc = pack.sub(2308, 2312)
        kv = A.alloc(512)
        pk = self.ps()
        for i in range(4):
            self.tr(pk.sub(i * 128, i * 128 + 128), cv.sub((2 + i) * T + c0, (2 + i) * T + c0 + 128))
        self.cp(kv, pk, eng='act')
        beta = A.alloc(4); la = A.alloc(4); gc = A.alloc(4); gl = A.alloc(4); ekd = A.alloc(4); bk = A.alloc(4)
        self.act(beta, tmv.sub(256, 260), AF.Sigmoid)
        self.tt(la, tmv.sub(260, 264), self.pr('dt_bias'), ALU.add)
        self.act(la, la, AF.Exp)
        self.act(la, la, AF.Ln, bias=1.0)
        self.act(gl, self.pr('a_log'), AF.Exp)
        self.tt(la, la, gl, ALU.mult)
        self.ts(la, la, -1.0, ALU.mult)
        pg = self.ps()
        self.mm(pg.sub(0, 4), self.c('tri'), la)
        self.cp(gc, pg.sub(0, 4))
        self.mm(pg.sub(8, 12), self.c('sel_last'), gc)
        self.cp(gl, pg.sub(8, 12))
        self.act(eg, gc, AF.Exp)
        self.tt(ekd, gl, gc, ALU.subtract)
        self.act(ekd, ekd, AF.Exp)
        self.tt(bk, beta, eg, ALU.mult)
        for ci in range(2):
            self.mm(pg.sub(16 + ci * 4, 20 + ci * 4), self.c('sel2_%d' % ci), gc)
            raw = pg.v(pg.ap[:, 16 + ci * 4:20 + ci * 4].rearrange("p (a b) -> p a b", a=2))
            for par in range(2):
                self.act(cdc.v(cdc.ap[par * 64:(par + 1) * 64, ci * 2:ci * 2 + 2]), raw.v(raw.ap[par * 64:(par + 1) * 64, :, par]), AF.Exp)
        dg_ = A.alloc(512)
        for h in range(4):
            self.ts(dg_.sub(h * 128, h * 128 + 128), self.c('ident'), gc.sub(h, h + 1), ALU.mult)
        pR = self.ps()
        self.mm(pR, self.c('ones'), dg_)
        EL = A.alloc(512); EU = attnT
        for h in range(4):
            hs = (h * 128, h * 128 + 128)
            self.ts(EL.sub(*hs), pR.sub(*hs), -1.0, ALU.mult, gc.sub(h, h + 1), ALU.add)
            self.ts(EU.sub(*hs), pR.sub(*hs), gc.sub(h, h + 1), ALU.subtract, 0.0, ALU.min)
        self.ts(EL, EL, 0.0, ALU.min)
        self.act(EL, EL, AF.Exp)
        self.act(EU, EU, AF.Exp)
        for h in range(4):
            hs = (h * 128, h * 128 + 128)
            self.tt(EL.sub(*hs), EL.sub(*hs), self.c('maskL'), ALU.mult)
            self.tt(EU.sub(*hs), EU.sub(*hs), self.c('maskI'), ALU.mult, eng='pool')
        self.memset(wT, 0.0, eng='pool')
        kz = A.alloc(512)
        Lm = [[A.alloc(128), A.alloc(128)] for _ in range(4)]
        Um = [[A.alloc(128), A.alloc(128)] for _ in range(4)]
        Q = [A.alloc(128) for _ in range(4)]
        for h in range(4):
            hp, par = h // 2, h % 2
            hs = (h * 128, h * 128 + 128)
            kt = cv.sub((2 + hp) * T + c0, (2 + hp) * T + c0 + 128)
            qt = cv.sub(hp * T + c0, hp * T + c0 + 128)
            mk = self.c('blk64').sub(par * 64, par * 64 + 1)
            self.ts(kz.sub(*hs), kt, mk, ALU.mult, eng='pool')
            self.ts(qz.sub(*hs), qt, mk, ALU.mult, eng='pool')
            pK = self.ps()
            self.mm(pK.sub(0, 128), kz.sub(*hs), kt)
            self.mm(pK.sub(128, 256), kz.sub(*hs), qt)
            self.stt(Lm[h][0], pK.sub(0, 128), beta.sub(h, h + 1), EL.sub(*hs), ALU.mult, ALU.mult)
            self.tt(attnT.sub(*hs), attnT.sub(*hs), pK.sub(128, 256), ALU.mult)
            self.tr(pK.sub(256, 384), Lm[h][0])
            self.cp(Um[h][0], pK.sub(256, 384), eng='act')
            self.tt(Q[h], self.c('ident'), Um[h][0], ALU.subtract)
        cur = 0
        for it in range(5):
            nxt = 1 - cur
            pps = []
            for h in range(4):
                pp = self.ps()
                pps.append(pp)
                self.mm(pp.sub(0, 128), Um[h][cur], Lm[h][cur])
                if it < 4:
                    self.mm(pp.sub(128, 256), Lm[h][cur], Um[h][cur])
            for h in range(4):
                pp = pps[h]
                self.cp(Lm[h][nxt], pp.sub(0, 128), eng='act')
                if it < 4:
                    self.cp(Um[h][nxt], pp.sub(128, 256), eng='act' if h % 2 else 'dve')
            for h in range(4):
                self.mm(pps[h].sub(256, 384), Lm[h][nxt], Q[h])
            for h in range(4):
                self.tt(Q[h], Q[h], pps[h].sub(256, 384), ALU.add)
            cur = nxt
        for h in range(4):
            hp, par = h // 2, h % 2
            vb = A.alloc(64); kbg = A.alloc(64)
            kh = kv.sub(h * 64, h * 64 + 64); vh = kv.sub(256 + h * 64, 256 + h * 64 + 64)
            self.ts(vb, vh, beta.sub(h, h + 1), ALU.mult)
            self.ts(kbg, kh, bk.sub(h, h + 1), ALU.mult)
            for ci_ in range(2):
                self.ts(kd.sub(ci_ * 256 + h * 64, ci_ * 256 + h * 64 + 64), kh, ekd.sub(h, h + 1), ALU.mult,
                        self.c('blk64').sub(ci_ * 64, ci_ * 64 + 1), ALU.mult, eng='pool')
            pu = self.ps()
            self.mm(pu.sub(0, 64), Q[h], vb)
            self.mm(pu.v(pu.ap[par * 64:(par + 1) * 64, 128:256]), kbg, Q[h])
            self.cp(uu.sub(h * 64, h * 64 + 64), pu.sub(0, 64), eng='act')
            self.cp(wT.v(wT.ap[par * 64:(par + 1) * 64, h * 128:h * 128 + 128]), pu.v(pu.ap[par * 64:(par + 1) * 64, 128:256]), eng='act')

    def gdn_rec(self, tl, pack, tmv, S, W, full, stgfm):
        A = self.A
        uu = pack.sub(0, 256); wT = pack.sub(256, 768); kd = pack.sub(768, 1280)
        attnT = pack.sub(1280, 1792); qz = pack.sub(1792, 2304); eg = pack.sub(2304, 2308); cdc = pack.sub(2308, 2312)
        S3 = S.r3(2)
        vnew = A.alloc(4 * W)
        vn3 = vnew.r3(4)
        self.memset(vnew, 0.0, eng='pool')
        o1 = A.alloc(256) if full else None
        for ci in range(2):
            rows = slice(ci * 64, ci * 64 + 64)
            pv_ = self.ps()
            for h in range(4):
                hp = h // 2
                self.mm(pv_.sub(h * W, (h + 1) * W), wT.sub(h * 128, h * 128 + 128), S.sub(hp * W, (hp + 1) * W))
            pv3 = pv_.v(pv_.ap[:, 0:4 * W].rearrange("p (a b) -> p a b", a=4))
            if full:
                self.tt(vnew.v(vn3.ap[rows, :, :]), uu.v(uu.r3(4).ap[rows, :, :]), pv3.v(pv3.ap[rows, :, :]), ALU.subtract)
                po = self.ps()
                for h in range(4):
                    hp = h // 2
                    self.mm(po.sub(h * 64, h * 64 + 64), qz.sub(h * 128, h * 128 + 128), S.sub(hp * W, (hp + 1) * W))
                po3 = po.v(po.ap[:, 0:256].rearrange("p (a b) -> p a b", a=4))
                egb = eg.v(eg.ap.unsqueeze(2).to_broadcast([128, 4, 64]))
                self.tt(o1.v(o1.r3(4).ap[rows, :, :]), po3.v(po3.ap[rows, :, :]), egb.v(egb.ap[rows, :, :]), ALU.mult)
            else:
                self.tt(vnew.v(vn3.ap[rows, :, 0:64]), uu.v(uu.r3(4).ap[rows, :, :]), pv3.v(pv3.ap[rows, :, 0:64]), ALU.subtract)
                self.ts(vnew.v(vn3.ap[rows, :, 64:128]), pv3.v(pv3.ap[rows, :, 64:128]), -1.0, ALU.mult)
            psn = self.ps()
            for h in range(4):
                hp, par = h // 2, h % 2
                prt = slice(par * 64, par * 64 + 64)
                self.mm(psn.v(psn.ap[prt, hp * W:(hp + 1) * W]), kd.sub(ci * 256 + h * 64, ci * 256 + h * 64 + 64), vnew.sub(h * W, (h + 1) * W))
            cdb = cdc.v(cdc.ap[:, ci * 2:ci * 2 + 2].unsqueeze(2).to_broadcast([128, 2, W]))
            self.tt(S3, S3, cdb, ALU.mult)
            self.tt(S, S, psn.sub(0, 2 * W), ALU.add)
        if full:
            po2 = self.ps()
            for h in range(4):
                self.mm(po2.sub(h * 64, h * 64 + 64), attnT.sub(h * 128, h * 128 + 128), vnew.sub(h * 64, h * 64 + 64))
            self.tt(o1, o1, po2.sub(0, 256), ALU.add)
            o2 = A.alloc(256); st4 = A.alloc(4); col = A.alloc(1)
            self.tt(o2, o1, o1, ALU.mult, eng='pool')
            self.red(st4, o2.r3(4))
            self.rsqrt(st4, st4, scale=1.0 / 64)
            self.tt(o1.r3(4), o1.r3(4), st4.v(st4.ap.unsqueeze(2).to_broadcast([128, 4, 64])), ALU.mult)
            gn = self.pr('gdn_norm')
            self.tt(o1.r3(4), o1.r3(4), gn.v(gn.ap.unsqueeze(1).to_broadcast([128, 4, 64])), ALU.mult)
            self.act(o2, tmv.sub(0, 256), AF.Silu)
            self.tt(o1, o1, o2, ALU.mult)
            self.branch_norm_tm(o1, 'bn3', o2, col)
            self.store_mixed_tm(o1, tl, 768, stgfm)

    def gdn_combine(self, S):
        A = self.A
        m = A.mark()
        sl = self.load_slots('gdn')
        self.memset(S, 0.0)
        Tbd = A.alloc(128); TbT = A.alloc(128); Sn = A.alloc(64)
        self.memset(Tbd, 0.0)
        for s in range(7):
            for hp in range(2):
                base = s * 256 + hp * 128
                for par in range(2):
                    prt = slice(par * 64, par * 64 + 64)
                    self.cp(Tbd.v(Tbd.ap[prt, par * 64:par * 64 + 64]), sl.v(sl.ap[prt, base + 64:base + 128]))
                pt = self.ps()
                self.tr(pt.sub(0, 128), Tbd)
                self.cp(TbT, pt.sub(0, 128), eng='act')
                self.mm(pt.sub(128, 192), TbT, S.sub(hp * 64, hp * 64 + 64))
                self.tt(Sn, pt.sub(128, 192), sl.sub(base, base + 64), ALU.add)
                self.blend(S.sub(hp * 64, hp * 64 + 64), Sn, s)
        A.release(m)

    def stage_post(self, dest):
        A = self.A
        m0 = A.mark()
        hT = A.alloc(8 * T)
        h3 = hT.r3(8)
        self.dma(h3, self.Dm(self.hx.rearrange("(j p) t -> p j t", p=128)[:, :, HALO:TH], self.hx_space))
        mo_ = A.mark()
        wo = A.alloc(8 * 512)
        wo3 = wo.r3(8)
        mx = [A.alloc(8 * 512), A.alloc(8 * 512)]
        wout = self.inp("w_out", [D, D]).rearrange("(j p) n -> p j n", p=128)
        for ng in range(2):
            self.dma(wo3, self.Dm(wout[:, :, ng * 512:(ng + 1) * 512], "w_out"))
            for blk in range(NB):
                mb = mx[blk % 2]
                self.dma(mb.r3(8), self.Dm(self.mixed.rearrange("(j p) t -> p j t", p=128)[:, :, blk * 512:(blk + 1) * 512], "mixed", 0, 1024))
                for n4 in range(4):
                    j = ng * 4 + n4
                    ps = self.ps()
                    for kc in range(8):
                        self.mm(ps, wo.v(wo3.ap[:, kc, n4 * 128:(n4 + 1) * 128]), mb.sub(kc * 512, (kc + 1) * 512), start=(kc == 0), stop=(kc == 7))
                    d = hT.v(h3.ap[:, j, blk * 512:(blk + 1) * 512])
                    self.tt(d, d, ps, ALU.add)
        if 'h1' in self.debug:
            self.dma(self.Dm(self.outp("h1", [D, T]).rearrange("(j p) t -> p j t", p=128), "h1o"), h3)
        A.release(mo_)
        hres = self.scratch("hres", [D, T])
        hresv = hres.rearrange("(j p) t -> p j t", p=128)
        self.dma(self.Dm(hresv, "hres", 0, 1 << 40), h3)
        A.release(m0)
        self.peer(hresv)
        hT = A.alloc(8 * T)
        h3 = hT.r3(8)
        self.dma(h3, self.Dm(hresv, "hres", 0, 1 << 40))
        if 'h2' in self.debug:
            self.dma(self.Dm(self.outp("h2", [D, T]).rearrange("(j p) t -> p j t", p=128), "h2o"), h3)
        m1 = A.mark()
        wg = A.alloc(8 * 512); wg3 = wg.r3(8)
        wp = A.alloc(2 * 512); wp3 = wp.r3(2)
        zb = A.alloc(8 * 512); pb = A.alloc(2 * 512); gt = A.alloc(512)
        wgd = self.inp("ple_wg", [D, D]).rearrange("(j p) n -> p j n", p=128)
        wpd = self.inp("ple_wp", [256, D]).rearrange("(j p) n -> p j n", p=128)
        pT = self.inp("pT", [256, T]).rearrange("(j p) t -> p j t", p=128)
        hn = A.alloc(8 * 512)
        for blk in range(NB):
            bs = slice(blk * 512, (blk + 1) * 512)
            self.rms_fm(lambda j: hT.v(h3.ap[:, j, bs]), 8, 512, 'ple_norm', lambda j: zb.sub(j * 512, (j + 1) * 512), 1.0 / D)
            self.dma(pb.r3(2), self.Dm(pT[:, :, bs], "pT"))
            for ng in range(2):
                self.dma(wg3, self.Dm(wgd[:, :, ng * 512:(ng + 1) * 512], "ple_wg"))
                self.dma(wp3, self.Dm(wpd[:, :, ng * 512:(ng + 1) * 512], "ple_wp"))
                for n4 in range(4):
                    j = ng * 4 + n4
                    pg = self.ps(); pp = self.ps()
                    for kc in range(8):
                        self.mm(pg, wg.v(wg3.ap[:, kc, n4 * 128:(n4 + 1) * 128]), zb.sub(kc * 512, (kc + 1) * 512), start=(kc == 0), stop=(kc == 7))
                    for kc in range(2):
                        self.mm(pp, wp.v(wp3.ap[:, kc, n4 * 128:(n4 + 1) * 128]), pb.sub(kc * 512, (kc + 1) * 512), start=(kc == 0), stop=(kc == 1))
                    self.act(gt, pg, AF.Sigmoid)
                    self.tt(gt, gt, pp, ALU.mult)
                    self.tt(hn.sub(j * 512, (j + 1) * 512), hT.v(h3.ap[:, j, bs]), gt, ALU.add)
            for j in range(8):
                self.cp(hT.v(h3.ap[:, j, bs]), hn.sub(j * 512, (j + 1) * 512), eng='pool' if j % 2 else 'act')
        A.release(m1)
        if dest == 'both':
            ho = self.outp("h_out", [D, T])
            self.dma(self.Dm(ho.rearrange("(j p) t -> p j t", p=128), "h_out"), h3)
        if dest == 'next':
            hx1 = self.scratch("hx1", [D, TH])
            hx1v = hx1.rearrange("(j p) t -> p j t", p=128)
            self.dma(self.Dm(hx1v[:, :, HALO:TH], "hx1"), h3)
            tail = self.scratch("tail", [128, 8 * HALO])
            gtail = self.scratch("gtail", [NCORES * 128, 8 * HALO])
            m1 = A.mark()
            tl_ = A.alloc(8 * HALO)
            self.cp(tl_.r3(8), hT.v(h3.ap[:, :, T - HALO:T]))
            self.dma(self.Dm(tail, "tail"), tl_)
            self.allgather(tail, "tail", gtail, "gtail")
            gt = A.alloc(NCORES * 8 * HALO)
            self.dma(gt.r3(NCORES), self.Dm(gtail.rearrange("(r p) n -> p r n", p=128), "gtail"))
            halo = A.alloc(8 * HALO)
            self.memset(halo, 0.0)
            for k_ in range(NCORES):
                self.stt(halo, gt.sub(k_ * 8 * HALO, (k_ + 1) * 8 * HALO), self.cmask.sub(8 + k_, 9 + k_), halo, ALU.mult, ALU.add)
            self.dma(self.Dm(hx1v[:, :, 0:HALO], "hx1"), halo.r3(8))
            A.release(m1)
            self.hx = hx1
            self.hx_space = "hx1"
        if dest in ('both', 'final'):
            m1 = A.mark()
            fo = self.outp("fin_out", [D, T]).rearrange("(j p) t -> p j t", p=128)
            fb = [A.alloc(8 * 512), A.alloc(8 * 512)]
            for blk in range(NB):
                bs = slice(blk * 512, (blk + 1) * 512)
                f = fb[blk % 2]
                self.rms_fm(lambda j: hT.v(h3.ap[:, j, bs]), 8, 512, 'final_norm', lambda j: f.sub(j * 512, (j + 1) * 512), 1.0 / D)
                self.dma(self.Dm(fo[:, :, bs], "fin_out"), f.r3(8))
            A.release(m1)
        A.release(m0)

    def peer(self, hresv):
        A = self.A
        m0 = A.mark()
        wq = self.inp("peer_wq", [D, D]).rearrange("(j p) n -> p j n", p=128)
        skd = self.inp("peer_sk", [8, 128, 256]).rearrange("h p n -> p h n")
        uvtab = self.inp("peer_uv", [16384, 2 * D])
        sk = A.alloc(8 * 256)
        self.dma(sk.r3(8), self.Dm(skd, "peer_sk"))
        hb = A.alloc(8 * 512); hb3 = hb.r3(8)
        zb = A.alloc(8 * 512)
        qTb = A.alloc(8 * 512)
        NSLOT = 128
        NGB = 10
        G = 4
        NDG = 8
        dg = [A.alloc(128) for _ in range(NDG)]
        ztm2 = [A.alloc(1024), A.alloc(1024)]
        eid2 = [A.alloc(NSLOT), A.alloc(NSLOT)]
        gates2 = [A.alloc(NSLOT), A.alloc(NSLOT)]
        self.AI.release(0)
        eidi2 = [self.AI.alloc(NSLOT), self.AI.alloc(NSLOT)]
        ti = self.AI.alloc(32)
        sc = A.alloc(256); wk = A.alloc(256); tv = A.alloc(32); tf = A.alloc(32)
        cand = A.alloc(256); cid = A.alloc(256); cw = A.alloc(256); top = A.alloc(16); ssum = A.alloc(1)
        actv = A.alloc(NSLOT); wts = A.alloc(NSLOT)
        gb = [A.alloc(2048) for _ in range(NGB)]
        wqb = gb[0].sub(0, 1024); wq3 = wqb.r3(8)
        self.nps = 6
        pacc = [self.PS[6], self.PS[7]]

        def head_gen(t4):
            par = t4 % 2
            ztm = ztm2[par]; eid = eid2[par]; gates = gates2[par]
            for half in range(2):
                pz = self.ps()
                for q in range(4):
                    j = half * 4 + q
                    self.tr(pz.sub(q * 128, q * 128 + 128), zb.v(zb.ap[:, j * 512 + t4 * 128:j * 512 + t4 * 128 + 128]))
                self.cp(ztm.sub(half * 512, half * 512 + 512), pz, eng='act')
            self.memset(eid, 0.0, eng='pool')
            yield
            for h in range(8):
                psc = self.ps()
                self.mm(psc.sub(0, 256), qTb.v(qTb.ap[:, h * 512 + t4 * 128:h * 512 + t4 * 128 + 128]), sk.sub(h * 256, h * 256 + 256))
                self.cp(sc, psc.sub(0, 256), eng='act')
                for side in range(2):
                    s_ = sc.sub(side * 128, side * 128 + 128); w_ = wk.sub(side * 128, side * 128 + 128)
                    o = side * 16
                    self.vmax(tv.sub(o, o + 8), s_)
                    self.vmaxi(ti.sub(o, o + 8), tv.sub(o, o + 8), s_)
                    self.vrepl(w_, tv.sub(o, o + 8), s_)
                    self.vmax(tv.sub(o + 8, o + 16), w_)
                    self.vmaxi(ti.sub(o + 8, o + 16), tv.sub(o + 8, o + 16), w_)
                self.cp(tf, ti, eng='pool')
                c3 = cand.r3(16)
                s1b = tv.v(tv.ap[:, 0:16].unsqueeze(2).to_broadcast([128, 16, 16]))
                s2b = tv.v(tv.ap[:, 16:32].unsqueeze(1).to_broadcast([128, 16, 16]))
                self.tt(c3, s1b, s2b, ALU.add)
                i1b = tf.v(tf.ap[:, 0:16].unsqueeze(2).to_broadcast([128, 16, 16]))
                i2b = tf.v(tf.ap[:, 16:32].unsqueeze(1).to_broadcast([128, 16, 16]))
                self.stt(cid.r3(16), i1b, 128.0, i2b, ALU.mult, ALU.add)
                self.vmax(top.sub(0, 8), cand)
                self.vrepl(cw, top.sub(0, 8), cand)
                self.vmax(top.sub(8, 16), cw)
                for k_ in range(16):
                    self.stt(cw, cand, top.sub(k_, k_ + 1), cid, ALU.is_equal, ALU.mult, accum=eid.sub(h * 16 + k_, h * 16 + k_ + 1))
                g = gates.sub(h * 16, h * 16 + 16)
                self.ts(g, top, top.sub(0, 1), ALU.subtract)
                self.memset(ssum, 0.0, eng='pool')
                self.act(g, g, AF.Exp, accum=ssum)
                self.recip(ssum, ssum)
                self.ts(g, g, ssum, ALU.mult)
                yield
            self.ts(eid, eid, 16383.0, ALU.min, 0.0, ALU.max)
            self.cp(eidi2[par], eid)
            yield

        def gather_gen(t4):
            par = t4 % 2
            ztm = ztm2[par]; gates = gates2[par]; eidi = eidi2[par]
            self.memset(actv, 0.0, eng='pool')
            for g0 in range(0, NSLOT, G):
                for s in range(g0, g0 + G):
                    gbuf = gb[s % NGB]
                    self.gather(gbuf, uvtab, eidi.sub(s, s + 1))
                    self.stt(gbuf.sub(0, 1024), gbuf.sub(0, 1024), 1.0, ztm, ALU.mult, ALU.mult, accum=actv.sub(s, s + 1))
                self.act(wts.sub(g0, g0 + G), actv.sub(g0, g0 + G), AF.Gelu)
                self.tt(wts.sub(g0, g0 + G), wts.sub(g0, g0 + G), gates.sub(g0, g0 + G), ALU.mult)
                for s in range(g0, g0 + G):
                    gbuf = gb[s % NGB]
                    d_ = dg[s % NDG]
                    self.act(d_, self.c('ident'), AF.Copy, scale=wts.sub(s, s + 1))
                    for half in range(2):
                        self.mm(pacc[half], d_, gbuf.sub(1024 + half * 512, 1536 + half * 512), start=(s == 0), stop=(s == NSLOT - 1))
                yield
            acc = ztm
            self.cp(acc.sub(0, 512), pacc[0], eng='act')
            self.cp(acc.sub(512, 1024), pacc[1])
            for half in range(2):
                pz = self.ps()
                for q in range(4):
                    j = half * 4 + q
                    self.tr(pz.sub(q * 128, q * 128 + 128), acc.sub(j * 128, j * 128 + 128))
                for q in range(4):
                    j = half * 4 + q
                    d = hb.v(hb3.ap[:, j, t4 * 128:(t4 + 1) * 128])
                    self.tt(d, d, pz.sub(q * 128, q * 128 + 128), ALU.add)

        for blk in range(NB):
            bs = slice(blk * 512, (blk + 1) * 512)
            self.dma(hb3, self.Dm(hresv[:, :, bs], "hres", blk, blk + 1))
            self.rms_fm(lambda j: hb.sub(j * 512, (j + 1) * 512), 8, 512, 'ffn_norm', lambda j: zb.sub(j * 512, (j + 1) * 512), 1.0 / D)
            for j in range(8):
                self.dma(wq3, self.Dm(wq[:, :, j * 128:(j + 1) * 128], "peer_wq"))
                ps = self.ps()
                for kc in range(8):
                    self.mm(ps, wqb.v(wq3.ap[:, kc, :]), zb.sub(kc * 512, (kc + 1) * 512), start=(kc == 0), stop=(kc == 7))
                self.cp(qTb.sub(j * 512, (j + 1) * 512), ps, eng='act')
            hgs = [head_gen(t4) for t4 in range(4)]
            for _ in hgs[0]:
                pass
            for t4 in range(4):
                hg = hgs[t4 + 1] if t4 < 3 else None
                for gi, _ in enumerate(gather_gen(t4)):
                    if hg is not None and gi % 3 == 1:
                        next(hg, None)
                if hg is not None:
                    for _ in hg:
                        pass
            self.dma(self.Dm(hresv[:, :, bs], "hres", blk, blk + 1), hb3)
        self.nps = 8
        self.AI.release(0)
        A.release(m0)

    def vmax(self, out, in_):
        self.P.op('dve', lambda e, o=out.ap, i=in_.ap: e.max(out=o, in_=i), reads=[in_], writes=[out])

    def vmaxi(self, out, mx, in_):
        u = out.ap.bitcast(U32)
        self.P.op('dve', lambda e, o=u, m=mx.ap, i=in_.ap: e.max_index(out=o, in_max=m, in_values=i), reads=[mx, in_], writes=[out])

    def vrepl(self, out, mx, in_):
        self.P.op('dve', lambda e, o=out.ap, m=mx.ap, i=in_.ap: e.match_replace(out=o, in_to_replace=m, in_values=i, imm_value=-1e30),
                  reads=[mx, in_], writes=[out])

    def gather(self, out, table, idx):
        self.P.op('pool', lambda e, o=out.ap, t=table, ix=idx.ap: e.indirect_dma_start(
            out=o, out_offset=None, in_=t, in_offset=bass.IndirectOffsetOnAxis(ap=ix, axis=0)),
            reads=[idx], writes=[out], dma=True)


_PROGS = {}


def get_prog(mode):
    if mode not in _PROGS:
        b = Builder(mode)
        b.build()
        _PROGS[mode] = b
    return _PROGS[mode]


def make_cmask():
    ms = []
    for c in range(NCORES):
        m = np.zeros((128, 16), np.float32)
        for k in range(NCORES):
            m[:, k] = 1.0 if k < c else 0.0
            m[:, 8 + k] = 1.0 if k == c - 1 else 0.0
        ms.append(m)
    return ms


def core_inputs(b, shared, percore):
    maps = []
    for c in range(NCORES):
        m = {}
        for name in b.din:
            m[name] = percore[name][c] if name in percore else shared[name]
        maps.append(m)
    return maps


def make_hx(hT_full):
    pad = np.concatenate([np.zeros((D, HALO), np.float32), hT_full], axis=1)
    return [np.ascontiguousarray(pad[:, c * T:c * T + TH]) for c in range(NCORES)]


def kernel(**inp):
    x = np.asarray(inp['x'], np.float32)[0]
    hT = np.ascontiguousarray(x.T)
    pos = np.asarray(inp['positions'], np.int32)
    shared = {'consts': make_consts()}
    percore = {'cmask': make_cmask(), 'hx': make_hx(hT),
               'pos': [np.ascontiguousarray(pos[:, c * T:(c + 1) * T]) for c in range(NCORES)]}
    for l in range(2):
        lp = make_layer_params(inp, l)
        for k_, v_ in lp.items():
            shared["l%d_%s" % (l, k_)] = v_
        pl = np.asarray(inp['p'], np.float32)[l, 0]
        percore["l%d_pT" % l] = [np.ascontiguousarray(pl[c * T:(c + 1) * T].T) for c in range(NCORES)]
    b = get_prog('fused')
    res = run_bass_kernel_spmd(b.nc, core_inputs(b, shared, percore), core_ids=list(range(NCORES))).results
    fin = np.concatenate([np.asarray(res[c]['fin_out'], np.float32) for c in range(NCORES)], axis=1)
    return np.ascontiguousarray(fin.T)[None].astype(np.float32)
```
